# Optimizing a Trainium2 kernel written in Bass

```python
import numpy as np
import jax, jax.numpy as jnp
from jax import lax

D_MODEL = 1024
BATCH = 2
SEQ = 8192
DEPTH = 2

GRID_W = 64
CTX_LEN = 256
EPS = 1e-6

NA_HEADS = 8
NA_HEAD_DIM = 64
NA_WIDTH = NA_HEADS * NA_HEAD_DIM
NA_KH = 8
NA_KW = 16

HG_HEADS = 4
HG_KEY_DIM = 128
HG_VAL_DIM = 128
HG_KEY = HG_HEADS * HG_KEY_DIM
HG_VAL = HG_HEADS * HG_VAL_DIM
HG_CHUNK = 32

D_FF = 2816
CONV_W = 3

N_EVEN = (DEPTH + 1) // 2
N_ODD = DEPTH // 2
EV_SIZES = [NA_WIDTH, NA_WIDTH, NA_WIDTH, HG_KEY, HG_KEY, HG_KEY, HG_VAL, HG_VAL]
EV_IN = sum(EV_SIZES)
EV_SPLITS = [int(s) for s in np.cumsum(EV_SIZES)[:-1]]
EV_MIX = NA_WIDTH + HG_VAL

kernel_name = "hybrid_na_hgrn2_shortconv_dit_block"


def rms_norm(x, w):
    xf = x.astype(jnp.float32)
    y = xf * lax.rsqrt(jnp.mean(xf * xf, axis=-1, keepdims=True) + EPS)
    return (y * w.astype(jnp.float32)).astype(x.dtype)


def modulate(h, shift, scale):
    return h * (1.0 + scale[:, None, :]) + shift[:, None, :]


def dwconv3(h, w, b):
    hp = jnp.pad(h, ((0, 0), (1, 1), (0, 0)))
    return hp[:, :-2] * w[0] + hp[:, 1:-1] * w[1] + hp[:, 2:] * w[2] + b


def neighbourhood_attention(q, k, v, k_ctx, v_ctx, rpb):
    b, s, h, d = q.shape
    rows = s // GRID_W
    kh = min(NA_KH, rows)
    kw = NA_KW
    scale = d ** -0.5
    cols = np.arange(GRID_W)
    col_start = np.clip(cols - kw // 2, 0, GRID_W - kw)
    col_idx = col_start[:, None] + np.arange(kw)[None, :]
    col_rel = col_idx - cols[:, None] + (NA_KW - 1)
    rpb_c = rpb.astype(jnp.float32)[:, :, col_rel]
    qg = q.reshape(b, rows, GRID_W, h, d)
    kg = k.reshape(b, rows, GRID_W, h, d)
    vg = v.reshape(b, rows, GRID_W, h, d)

    def row_block(args):
        r, q_row = args
        r0 = jnp.clip(r - kh // 2, 0, rows - kh)
        k_rows = lax.dynamic_slice_in_dim(kg, r0, kh, axis=1)
        v_rows = lax.dynamic_slice_in_dim(vg, r0, kh, axis=1)
        k_nb = k_rows[:, :, col_idx]
        v_nb = v_rows[:, :, col_idx]
        row_rel = r0 + jnp.arange(kh) - r + (NA_KH - 1)
        bias = rpb_c[:, row_rel].transpose(0, 2, 1, 3)
        s_loc = jnp.einsum('bqhd,brqwhd->bhqrw', q_row, k_nb).astype(jnp.float32) * scale + bias[None]
        s_ctx = jnp.einsum('bqhd,bkhd->bhqk', q_row, k_ctx).astype(jnp.float32) * scale
        logits = jnp.concatenate([s_loc.reshape(b, h, GRID_W, kh * kw), s_ctx], axis=-1)
        p = jax.nn.softmax(logits, axis=-1).astype(v.dtype)
        p_loc = p[..., :kh * kw].reshape(b, h, GRID_W, kh, kw)
        p_ctx = p[..., kh * kw:]
        return (jnp.einsum('bhqrw,brqwhd->bqhd', p_loc, v_nb)
                + jnp.einsum('bhqk,bkhd->bqhd', p_ctx, v_ctx))

    out = lax.map(row_block, (jnp.arange(rows), qg.transpose(1, 0, 2, 3, 4)))
    return out.transpose(1, 0, 2, 3, 4).reshape(b, s, h, d)


def context_attention(q, k, v):
    s = jnp.einsum('bqhd,bkhd->bhqk', q, k).astype(jnp.float32) * (q.shape[-1] ** -0.5)
    p = jax.nn.softmax(s, axis=-1).astype(v.dtype)
    return jnp.einsum('bhqk,bkhd->bqhd', p, v)


def hgrn2_scan(q, k, v, log_f, s0):
    b, h, l, _ = q.shape
    n = l // HG_CHUNK

    def to_chunks(t):
        return t.reshape(b, h, n, HG_CHUNK, t.shape[-1]).transpose(2, 0, 1, 3, 4)

    causal_in_chunk = jnp.tril(jnp.ones((HG_CHUNK, HG_CHUNK), dtype=bool))

    def step(state, inp):
        qc, kc, vc, gc = inp
        bcum = jnp.cumsum(gc, axis=2)
        btot = bcum[:, :, -1:]
        qe = qc * jnp.exp(bcum)
        ke = kc * jnp.exp(-bcum)
        att = jnp.where(causal_in_chunk, jnp.einsum('bhtk,bhsk->bhts', qe, ke), 0.0)
        o = jnp.einsum('bhtk,bhkv->bhtv', qe, state) + jnp.einsum('bhts,bhsv->bhtv', att, vc)
        kd = kc * jnp.exp(btot - bcum)
        new_state = jnp.exp(btot)[:, :, 0, :, None] * state + jnp.einsum('bhsk,bhsv->bhkv', kd, vc)
        return new_state, o

    s_end, o = lax.scan(step, s0, (to_chunks(q), to_chunks(k), to_chunks(v), to_chunks(log_f)))
    return o.transpose(1, 2, 0, 3, 4).reshape(b, h, l, v.shape[-1]), s_end


def hgrn2_final_state(k, v, log_f):
    bcum = jnp.cumsum(log_f, axis=2)
    return jnp.einsum('bhtk,bhtv->bhkv', k * jnp.exp(bcum[:, :, -1:] - bcum), v)


def _heads(a, dh):
    b, l, _ = a.shape
    return a.astype(jnp.float32).reshape(b, l, -1, dh).transpose(0, 2, 1, 3)


def _hg_gates(f_pres, i, lb):
    v = _heads(i, HG_VAL_DIM)
    dirs = []
    for d, f_pre in enumerate(f_pres):
        lbd = lb[d].astype(jnp.float32).reshape(HG_HEADS, 1, HG_KEY_DIM)
        fg = lbd + (1.0 - lbd) * jax.nn.sigmoid(_heads(f_pre, HG_KEY_DIM))
        dirs.append((1.0 - fg, jnp.log(fg)))
    return v, dirs


def _hg_readout(o, g, norm_w):
    b, _, l, _ = o.shape
    o = o * lax.rsqrt(jnp.mean(o * o, axis=-1, keepdims=True) + EPS)
    o = o * norm_w.astype(jnp.float32).reshape(HG_HEADS, 1, HG_VAL_DIM)
    o = o.transpose(0, 2, 1, 3).reshape(b, l, HG_VAL)
    return (o * jax.nn.silu(g.astype(jnp.float32))).astype(g.dtype)


def hgrn2_mixer(q, fs, i, g, cq, cfs, ci, cg, lb, norm_w, with_ctx_out):
    v, dirs = _hg_gates(fs, i, lb)
    vc, dirs_c = _hg_gates(cfs, ci, lb)
    qh = jax.nn.silu(_heads(q, HG_KEY_DIM))
    if with_ctx_out:
        cqh = jax.nn.silu(_heads(cq, HG_KEY_DIM))
    b = q.shape[0]
    zero_state = jnp.zeros((b, HG_HEADS, HG_KEY_DIM, HG_VAL_DIM), jnp.float32)
    o_lat = jnp.zeros_like(qh[..., :HG_VAL_DIM])
    o_ctx = None
    for d in range(2):
        tr = (lambda a: jnp.flip(a, axis=2)) if d == 1 else (lambda a: a)
        k, lf = dirs[d]
        kc, lfc = dirs_c[d]
        if with_ctx_out:
            oc, s_ctx = hgrn2_scan(tr(cqh), tr(kc), tr(vc), tr(lfc), zero_state)
            o_ctx = tr(oc) if o_ctx is None else o_ctx + tr(oc)
        else:
            s_ctx = hgrn2_final_state(tr(kc), tr(vc), tr(lfc))
        ol, _ = hgrn2_scan(tr(qh), tr(k), tr(v), tr(lf), s_ctx)
        o_lat = o_lat + tr(ol)
    y = _hg_readout(o_lat, g, norm_w)
    y_ctx = _hg_readout(o_ctx, cg, norm_w) if with_ctx_out else None
    return y, y_ctx


def even_mixer(h, h_ctx, w_in, w_out, rpb, lb, hg_norm_w, with_ctx_out):
    b, s, _ = h.shape
    na_q, na_k, na_v, hg_q, hg_ff, hg_fb, hg_i, hg_g = jnp.split(h @ w_in, EV_SPLITS, axis=-1)
    cna_q, cna_k, cna_v, chg_q, chg_ff, chg_fb, chg_i, chg_g = jnp.split(h_ctx @ w_in, EV_SPLITS, axis=-1)

    def nh(a):
        return a.reshape(a.shape[0], a.shape[1], NA_HEADS, NA_HEAD_DIM)

    a_lat = neighbourhood_attention(nh(na_q), nh(na_k), nh(na_v), nh(cna_k), nh(cna_v), rpb)
    g_lat, g_ctx = hgrn2_mixer(hg_q, (hg_ff, hg_fb), hg_i, hg_g,
                               chg_q, (chg_ff, chg_fb), chg_i, chg_g, lb, hg_norm_w, with_ctx_out)
    y = jnp.concatenate([a_lat.reshape(b, s, NA_WIDTH), g_lat], axis=-1) @ w_out
    y_ctx = None
    if with_ctx_out:
        a_ctx = context_attention(nh(cna_q), nh(cna_k), nh(cna_v))
        y_ctx = jnp.concatenate([a_ctx.reshape(b, h_ctx.shape[1], NA_WIDTH), g_ctx], axis=-1) @ w_out
    return y, y_ctx


def short_conv_mixer(h, w_in, conv_w, conv_b, w_out):
    gate_b, gate_c, u = jnp.split(h @ w_in, 3, axis=-1)
    return (gate_b * dwconv3(gate_c * u, conv_w, conv_b)) @ w_out


def conv_ffn(h, w_up, conv_w, conv_b, w_down):
    a, val = jnp.split(h @ w_up, 2, axis=-1)
    return (jax.nn.gelu(dwconv3(a, conv_w, conv_b), approximate=False) * val) @ w_down


def setup_inputs(seed: int = 0) -> dict:
    key = jax.random.key(seed)
    ks = jax.random.split(key, 24)
    f32 = jnp.float32
    nrm = lambda k, shape, s: jax.random.normal(k, shape, f32) * s
    D = D_MODEL
    return {
        "x": nrm(ks[0], (BATCH, SEQ, D), 1.0),
        "c": nrm(ks[1], (BATCH, D), 1.0),
        "ctx": nrm(ks[2], (BATCH, CTX_LEN, D), 1.0),
        "c_ctx": nrm(ks[3], (D,), 1.0),
        "ada_w": nrm(ks[4], (DEPTH, D, 6 * D), 0.5 * D ** -0.5),
        "ada_b": nrm(ks[5], (DEPTH, 6 * D), 0.02),
        "norm_mix_w": 1.0 + nrm(ks[6], (DEPTH, D), 0.02),
        "norm_ffn_w": 1.0 + nrm(ks[7], (DEPTH, D), 0.02),
        "ev_w_in": nrm(ks[8], (N_EVEN, D, EV_IN), D ** -0.5),
        "ev_w_out": nrm(ks[9], (N_EVEN, EV_MIX, D), EV_MIX ** -0.5),
        "na_rpb": nrm(ks[10], (N_EVEN, NA_HEADS, 2 * NA_KH - 1, 2 * NA_KW - 1), 0.1),
        "hg_lb_logits": nrm(ks[11], (N_EVEN + 1, 2, HG_KEY), 0.1),
        "hg_norm_w": 1.0 + nrm(ks[12], (N_EVEN, HG_VAL), 0.02),
        "od_w_in": nrm(ks[13], (N_ODD, D, 3 * D), D ** -0.5),
        "od_conv_w": nrm(ks[14], (N_ODD, CONV_W, D), CONV_W ** -0.5),
        "od_conv_b": nrm(ks[15], (N_ODD, D), 0.02),
        "od_w_out": nrm(ks[16], (N_ODD, D, D), D ** -0.5),
        "ffn_w_up": nrm(ks[17], (DEPTH, D, 2 * D_FF), D ** -0.5),
        "ffn_conv_w": nrm(ks[18], (DEPTH, CONV_W, D_FF), CONV_W ** -0.5),
        "ffn_conv_b": nrm(ks[19], (DEPTH, D_FF), 0.02),
        "ffn_w_down": nrm(ks[20], (DEPTH, D_FF, D), D_FF ** -0.5),
        "final_norm_w": 1.0 + nrm(ks[21], (D,), 0.02),
    }


def reference(x, c, ctx, c_ctx, ada_w, ada_b, norm_mix_w, norm_ffn_w, ev_w_in, ev_w_out,
              na_rpb, hg_lb_logits, hg_norm_w, od_w_in, od_conv_w, od_conv_b, od_w_out,
              ffn_w_up, ffn_conv_w, ffn_conv_b, ffn_w_down, final_norm_w):
    lb_all = jnp.cumsum(jax.nn.softmax(hg_lb_logits.astype(jnp.float32), axis=0), axis=0)[:N_EVEN]
    silu_c = jax.nn.silu(c)
    silu_cc = jax.nn.silu(c_ctx)[None]
    xc = ctx
    for layer in range(DEPTH):
        j = layer // 2
        is_even = layer % 2 == 0
        ctx_feeds_later = any(l % 2 == 0 for l in range(layer + 1, DEPTH))
        mod = silu_c @ ada_w[layer] + ada_b[layer]
        shift_m, scale_m, gate_m, shift_f, scale_f, gate_f = jnp.split(mod, 6, axis=-1)
        if is_even or ctx_feeds_later:
            mod_c = silu_cc @ ada_w[layer] + ada_b[layer]
            cshift_m, cscale_m, cgate_m, cshift_f, cscale_f, cgate_f = jnp.split(mod_c, 6, axis=-1)
            h_ctx = modulate(rms_norm(xc, norm_mix_w[layer]), cshift_m, cscale_m)
        h = modulate(rms_norm(x, norm_mix_w[layer]), shift_m, scale_m)
        if is_even:
            y, y_ctx = even_mixer(h, h_ctx, ev_w_in[j], ev_w_out[j], na_rpb[j], lb_all[j],
                                  hg_norm_w[j], ctx_feeds_later)
        else:
            y = short_conv_mixer(h, od_w_in[j], od_conv_w[j], od_conv_b[j], od_w_out[j])
            y_ctx = (short_conv_mixer(h_ctx, od_w_in[j], od_conv_w[j], od_conv_b[j], od_w_out[j])
                     if ctx_feeds_later else None)
        x = x + gate_m[:, None, :] * y
        h = modulate(rms_norm(x, norm_ffn_w[layer]), shift_f, scale_f)
        x = x + gate_f[:, None, :] * conv_ffn(h, ffn_w_up[layer], ffn_conv_w[layer],
                                              ffn_conv_b[layer], ffn_w_down[layer])
        if ctx_feeds_later:
            xc = xc + cgate_m[:, None, :] * y_ctx
            hc = modulate(rms_norm(xc, norm_ffn_w[layer]), cshift_f, cscale_f)
            xc = xc + cgate_f[:, None, :] * conv_ffn(hc, ffn_w_up[layer], ffn_conv_w[layer],
                                                     ffn_conv_b[layer], ffn_w_down[layer])
    return rms_norm(x, final_norm_w)
```

```python
from contextlib import ExitStack
import numpy as np
import ml_dtypes
import concourse.bass as bass
import concourse.mybir as mybir
from concourse.bass_utils import run_bass_kernel_spmd

F32 = mybir.dt.float32
BF16 = mybir.dt.bfloat16
AF = mybir.ActivationFunctionType
ALU = mybir.AluOpType

SAME_ENGINE_SYNC = True
SEM_GEN = 30000
EPS = 1e-6
NCORES = 8
D = 1024
SEQ = 8192
SEG = 2048
HALO = 4
NT = SEG + 2 * HALO
DFF = 2816


class Tok:
    __slots__ = ("w", "r")

    def __init__(self):
        self.w = None
        self.r = []


class Prog:
    ENGS = ("pe", "act", "dve", "pool", "sp")

    def __init__(self, nc, stack):
        self.nc = nc
        self.stack = stack
        self.lists = {e: [] for e in self.ENGS}
        self.cur_sem = {}
        self.cnt = {}
        self.seen = {e: {} for e in self.ENGS}
        self.pending = {e: False for e in self.ENGS}
        self.nsem = 0
        for e in ("pe", "act", "dve", "pool"):
            self._new_sem(e)
        self.dma_pool = {"sp": [[self._alloc_sem(), 0] for _ in range(16)]}
        self.dma_rr = {"sp": 0}
        self.n_ins = 0

    def _alloc_sem(self):
        self.nsem += 1
        return self.stack.enter_context(self.nc.semaphore("s%d" % self.nsem))

    def _new_sem(self, e):
        self.cur_sem[e] = self._alloc_sem()
        self.cnt[e] = 0

    def _need(self, e, ev):
        if ev is None:
            return
        feng, sem, val = ev
        if feng == e and (e == "pe" or not SAME_ENGINE_SYNC):
            return
        k = id(sem)
        if self.seen[e].get(k, 0) >= val:
            return
        self.seen[e][k] = val
        self.lists[e].append(("wait", sem, val))

    def _deps(self, e, reads, writes):
        for t in reads:
            self._need(e, t.w)
        for t in writes:
            self._need(e, t.w)
            for ev in t.r:
                self._need(e, ev)

    def _record(self, ev, reads, writes):
        for t in reads:
            t.r.append(ev)
            if len(t.r) > 64:
                t.r = t.r[-48:]
        for t in writes:
            t.w = ev
            t.r = []

    def op(self, e, fn, reads=(), writes=(), inc=True):
        assert inc or e == "pe"
        self._deps(e, reads, writes)
        if inc and self.cnt[e] >= SEM_GEN and not self.pending[e]:
            self._new_sem(e)
        sem = self.cur_sem[e]
        val = self.cnt[e] + 1
        if inc:
            self.cnt[e] = val
            self.pending[e] = False
        else:
            self.pending[e] = True
        ev = (e, sem, val)
        self.lists[e].append(("ins", fn, sem if inc else None, 1))
        self._record(ev, reads, writes)
        self.n_ins += 1
        return ev

    def I(self, e, meth, reads=(), writes=(), inc=True, **kw):
        def fn(eng, meth=meth, kw=kw):
            return getattr(eng, meth)(**kw)
        return self.op(e, fn, reads=reads, writes=writes, inc=inc)

    def dma(self, out, in_, reads=(), writes=(), q="sp"):
        pool = self.dma_pool[q]
        i = self.dma_rr[q]
        self.dma_rr[q] = (i + 1) % len(pool)
        slot = pool[i]
        sem = slot[0]
        if slot[1] > 0:
            self._need(q, ("dma", sem, slot[1]))
        self._deps(q, reads, writes)
        slot[1] += 16
        ev = ("dma", sem, slot[1])

        def fn(eng, out=out, in_=in_):
            return eng.dma_start(out=out, in_=in_)

        self.lists[q].append(("ins", fn, sem, 16))
        self._record(ev, reads, writes)
        self.n_ins += 1
        return ev

    def barrier(self):
        evs = []
        for x in ("pe", "act", "dve", "pool"):
            assert not self.pending[x], x
            if self.cnt[x] > 0:
                evs.append((x, self.cur_sem[x], self.cnt[x]))
        for q, pool in self.dma_pool.items():
            for sem, v in pool:
                if v > 0:
                    evs.append(("dma", sem, v))
        for e in self.ENGS:
            for ev in evs:
                if ev[0] == e and e != "dma":
                    if e == "pe" or not SAME_ENGINE_SYNC:
                        continue
                self._need(e, ev)

    def finish(self, e="sp"):
        for x in ("pe", "act", "dve", "pool"):
            assert not self.pending[x], x
            if self.cnt[x] > 0:
                self._need(e, (x, self.cur_sem[x], self.cnt[x]))
        for q, pool in self.dma_pool.items():
            for sem, v in pool:
                if v > 0:
                    self._need(e, ("dma", sem, v))

    def emit(self):
        nc = self.nc
        lists = self.lists
        with nc.Block() as block:
            def run(eng, lst):
                for it in lst:
                    if it[0] == "wait":
                        eng.wait_ge(it[1], it[2])
                    else:
                        ins = it[1](eng)
                        if it[2] is not None:
                            ins.then_inc(it[2], it[3])

            @block.tensor
            def _(eng):
                run(eng, lists["pe"])

            @block.scalar
            def _(eng):
                run(eng, lists["act"])

            @block.vector
            def _(eng):
                run(eng, lists["dve"])

            @block.gpsimd
            def _(eng):
                run(eng, lists["pool"])

            @block.sync
            def _(eng):
                run(eng, lists["sp"])


WELEMS = 1408


class KB:
    def __init__(self, nc, st, n_wst=3, n_wbf=4):
        self.nc = nc
        self.st = st
        self.P = Prog(nc, st)
        self.banks = [st.enter_context(nc.psum_tensor("bank%d" % i, [128, 512], F32)) for i in range(8)]
        self.btok = [Tok() for _ in range(8)]
        self.bi = 0
        self.nring = 8
        self.ring0 = 0
        self.wst = [self.sb("wst%d" % i, [128, WELEMS], F32) for i in range(n_wst)]
        self.wst_tok = [Tok() for _ in range(n_wst)]
        self.wbf = [self.sb("wbf%d" % i, [128, WELEMS], BF16) for i in range(n_wbf)]
        self.wbf_tok = [Tok() for _ in range(n_wbf)]
        self.rr = {}
        self.ones32 = self.sb("ones32", [128, 128], F32)
        self.ones_tok = Tok()
        self.P.I("pool", "memset", [], [self.ones_tok], ap=self.ones32[:], constant=1.0)
        self.eps_sb = self.sb("eps_sb", [128, 1], F32)
        self.eps_tok = Tok()
        self.P.I("pool", "memset", [], [self.eps_tok], ap=self.eps_sb[:], constant=EPS)

    def sb(self, name, shape, dt):
        return self.st.enter_context(self.nc.sbuf_tensor(name, shape, dt))

    def dram_in(self, name, shape, dt=F32):
        return self.nc.dram_tensor(name, list(shape), dt, kind="ExternalInput").ap()

    def dram_out(self, name, shape, dt=F32):
        return self.nc.dram_tensor(name, list(shape), dt, kind="ExternalOutput").ap()

    def set_ring(self, start, n):
        self.ring0 = start
        self.nring = n
        self.bi = 0

    def bank(self):
        i = self.ring0 + self.bi
        self.bi = (self.bi + 1) % self.nring
        return self.banks[i], self.btok[i]

    def rot(self, key, n):
        i = self.rr.get(key, 0)
        self.rr[key] = (i + 1) % n
        return i

    def load_small(self, name, ap_dram, shape, dt=F32):
        t = self.sb(name, shape, dt)
        tk = Tok()
        self.P.dma(t[:], ap_dram, writes=[tk])
        return t, tk

    def load_w(self, W, k0, KC, c0, n=128, cast=True):
        P = self.P
        assert KC * n <= WELEMS
        i = self.rot("wst", len(self.wst))
        stg_flat = self.wst[i][:, 0:KC * n]
        stg = stg_flat.rearrange("p (kc n) -> p kc n", kc=KC)
        src = W[k0 * 128:(k0 + KC) * 128, c0:c0 + n].rearrange("(kc p) n -> p kc n", p=128)
        P.dma(stg, src, writes=[self.wst_tok[i]])
        if not cast:
            return stg, self.wst_tok[i]
        j = self.rot("wbf", len(self.wbf))
        dst = self.wbf[j][:, 0:KC * n]
        P.I("pool", "tensor_copy", [self.wst_tok[i]], [self.wbf_tok[j]], out=dst, in_=stg_flat)
        return dst.rearrange("p (kc n) -> p kc n", kc=KC), self.wbf_tok[j]

    def mm_acc(self, ps_ap, ptok, pairs, reads):
        n = len(pairs)
        for i, (l, r) in enumerate(pairs):
            self.P.I("pe", "matmul", reads, [ptok], inc=(i == n - 1), out=ps_ap, lhsT=l, rhs=r, start=(i == 0), stop=(i == n - 1))


def subblocks(c0, c1, maxw=512):
    L = c1 - c0
    n = (L + maxw - 1) // maxw
    base = L // n
    rem = L % n
    out = []
    s = c0
    for i in range(n):
        w = base + (1 if i < rem else 0)
        out.append((s, s + w))
        s += w
    return out


def compute_mod(kb, ada_w, ada_b_sb, ada_b_tok, scT, scT_tok, nrhs, chunks, out, out_tok):
    P = kb.P
    for n in chunks:
        w, wt = kb.load_w(ada_w, 0, 8, n * 128, 128, cast=False)
        bk, bt = kb.bank()
        kb.mm_acc(bk[:, 0:nrhs], bt, [(w[:, kc, :], scT[:, kc, :]) for kc in range(8)], [wt, scT_tok])
        for r in range(nrhs):
            P.I("dve", "tensor_tensor", [bt, ada_b_tok], [out_tok], out=out[:, r, n:n + 1], in0=bk[:, r:r + 1], in1=ada_b_sb[:, n:n + 1], op=ALU.add)


def load_xT(kb, x_dram, ntok, xT, xtok, ident, ident_tok, xt, xtt):
    P = kb.P
    nt = (ntok + 127) // 128
    for i in range(nt):
        r0 = i * 128
        rows = min(128, ntok - r0)
        b = i % 2
        P.dma(xt[b][0:rows, 0:1024], x_dram[r0:r0 + rows, :], writes=[xtt[b]])
        for half in range(2):
            bk, bt = kb.bank()
            for q in range(4):
                kc = half * 4 + q
                P.I("pe", "transpose", [xtt[b], ident_tok], [bt], inc=(q == 3), out=bk[:, q * 128:q * 128 + rows],
                    in_=xt[b][0:rows, kc * 128:(kc + 1) * 128], identity=ident[0:rows, 0:rows])
            src = bk[:, :].rearrange("p (q n) -> p q n", q=4)[:, :, 0:rows]
            dst = xT[:, half * 4:half * 4 + 4, r0:r0 + rows]
            if half == 0:
                P.I("act", "copy", [bt], [xtok], out=dst, in_=src)
            else:
                P.I("dve", "tensor_copy", [bt], [xtok], out=dst, in_=src)


def norm_mod(kb, xT, xtoks, c0, c1, wv, bv, vtoks, hT, htoks, hoff=0, maxw=512):
    P = kb.P
    if not hasattr(kb, "nm_sq"):
        kb.nm_sq = [kb.sb("nmsq%d" % i, [128, 512], F32) for i in range(2)]
        kb.nm_sq_tok = [Tok() for _ in range(2)]
        kb.nm_r = [kb.sb("nmr%d" % i, [128, 512], F32) for i in range(2)]
        kb.nm_r_tok = [Tok() for _ in range(2)]
        kb.nm_t = [kb.sb("nmt%d" % i, [128, 512], F32) for i in range(2)]
        kb.nm_t_tok = [Tok() for _ in range(2)]
    for (s0, s1) in subblocks(c0, c1, maxw):
        w = s1 - s0
        bk, bt = kb.bank()
        for kc in range(8):
            i = kb.rot("nmsq", 2)
            sq = kb.nm_sq[i]
            P.I("act", "activation", xtoks, [kb.nm_sq_tok[i]], out=sq[:, 0:w], in_=xT[:, kc, s0:s1], func=AF.Square)
            P.I("pe", "matmul", [kb.nm_sq_tok[i], kb.ones_tok], [bt], out=bk[:, 0:w], lhsT=kb.ones32[:], rhs=sq[:, 0:w], start=(kc == 0), stop=(kc == 7))
        ri = kb.rot("nmr", 2)
        r = kb.nm_r[ri]
        rt = kb.nm_r_tok[ri]
        P.I("act", "activation", [bt, kb.eps_tok], [rt], out=r[:, 0:w], in_=bk[:, 0:w], func=AF.Sqrt, bias=kb.eps_sb[:, 0:1], scale=1.0 / D)
        P.I("dve", "reciprocal", [rt], [rt], out=r[:, 0:w], in_=r[:, 0:w])
        for kc in range(8):
            dst = hT[:, kc, hoff + s0 - c0:hoff + s1 - c0]
            if bv is None:
                P.I("dve", "scalar_tensor_tensor", list(xtoks) + [rt] + list(vtoks), htoks, out=dst, in0=xT[:, kc, s0:s1], scalar=wv[:, kc:kc + 1], in1=r[:, 0:w], op0=ALU.mult, op1=ALU.mult)
            else:
                ti = kb.rot("nmt", 2)
                t = kb.nm_t[ti]
                tt = kb.nm_t_tok[ti]
                P.I("dve", "scalar_tensor_tensor", list(xtoks) + [rt] + list(vtoks), [tt], out=t[:, 0:w], in0=xT[:, kc, s0:s1], scalar=wv[:, kc:kc + 1], in1=r[:, 0:w], op0=ALU.mult, op1=ALU.mult)
                P.I("pool", "tensor_scalar", [tt] + list(vtoks), htoks, out=dst, in0=t[:, 0:w], scalar1=bv[:, kc:kc + 1], scalar2=None, op0=ALU.add)


def glu_conv_block(kb, kind, nwv, nbv, nvtoks, W1, W2, KC2, cw, cb, cvtok, gate, gtok, xT, xtok, mvec, mvtok, hT, htok, U, utok):
    P = kb.P
    halves = [(0, 1030, 1, 1028), (1026, NT, 1028, NT - 1)]

    def up(hidx):
        c0, c1, o0, o1 = halves[hidx]
        L = c1 - c0
        sbs = subblocks(c0, c1, 344)
        for j in range(KC2):
            ai = kb.rot("gca", 2)
            abuf, atok = kb.gc_a[ai], kb.gc_a_tok[ai]
            ti = kb.rot("gct", 2)
            tbuf, ttok = kb.gc_t[ti], kb.gc_t_tok[ti]
            bbuf, btok2 = kb.gc_b, kb.gc_b_tok

            def proj(col):
                w, wt = kb.load_w(W1, 0, 8, col * 128, 128)
                outs = []
                for (s0, s1) in sbs:
                    bk, bt = kb.bank()
                    kb.mm_acc(bk[:, 0:s1 - s0], bt, [(w[:, kc, :], hT[:, kc, s0 - c0:s1 - c0]) for kc in range(8)], [wt, htok])
                    outs.append((bk, bt, s0, s1))
                return outs

            def edge_mask(buf, tok):
                if hidx == 0:
                    P.I("pool", "tensor_scalar", [tok, mvtok], [tok], out=buf[:, 0:HALO], in0=buf[:, 0:HALO], scalar1=mvec[:, 0:1], scalar2=None, op0=ALU.mult)
                else:
                    P.I("pool", "tensor_scalar", [tok, mvtok], [tok], out=buf[:, L - HALO:L], in0=buf[:, L - HALO:L], scalar1=mvec[:, 1:2], scalar2=None, op0=ALU.mult)

            def conv(src, srct, dst, dstt):
                P.I("act", "activation", [srct, cvtok], [dstt], out=dst[:, 1:L - 1], in_=src[:, 1:L - 1], func=AF.Identity, bias=cb[:, j:j + 1], scale=cw[:, j, 1:2])
                P.I("dve", "scalar_tensor_tensor", [srct, cvtok], [dstt], out=dst[:, 1:L - 1], in0=src[:, 0:L - 2], scalar=cw[:, j, 0:1], in1=dst[:, 1:L - 1], op0=ALU.mult, op1=ALU.add)
                P.I("dve", "scalar_tensor_tensor", [srct, cvtok], [dstt], out=dst[:, 1:L - 1], in0=src[:, 2:L], scalar=cw[:, j, 2:3], in1=dst[:, 1:L - 1], op0=ALU.mult, op1=ALU.add)

            if kind == "ffn":
                for (bk, bt, s0, s1) in proj(j):
                    P.I("act", "copy", [bt], [atok], out=abuf[:, s0 - c0:s1 - c0], in_=bk[:, 0:s1 - s0])
                edge_mask(abuf, atok)
                conv(abuf, atok, tbuf, ttok)
                P.I("act", "activation", [ttok], [ttok], out=tbuf[:, 1:L - 1], in_=tbuf[:, 1:L - 1], func=AF.Gelu)
                for (bk, bt, s0, s1) in proj(KC2 + j):
                    a0 = max(s0, c0 + 1)
                    a1 = min(s1, c1 - 1)
                    P.I("dve", "tensor_tensor", [bt, ttok], [utok], out=U[:, j, a0 - c0:a1 - c0], in0=tbuf[:, a0 - c0:a1 - c0], in1=bk[:, a0 - s0:a1 - s0], op=ALU.mult)
            else:
                for (bk, bt, s0, s1) in proj(j):
                    P.I("act", "copy", [bt], [btok2], out=bbuf[:, s0 - c0:s1 - c0], in_=bk[:, 0:s1 - s0])
                for (bk, bt, s0, s1) in proj(KC2 + j):
                    P.I("act", "copy", [bt], [ttok], out=tbuf[:, s0 - c0:s1 - c0], in_=bk[:, 0:s1 - s0])
                edge_mask(tbuf, ttok)
                for (bk, bt, s0, s1) in proj(2 * KC2 + j):
                    P.I("dve", "tensor_tensor", [bt, ttok], [atok], out=abuf[:, s0 - c0:s1 - c0], in0=tbuf[:, s0 - c0:s1 - c0], in1=bk[:, 0:s1 - s0], op=ALU.mult)
                conv(abuf, atok, tbuf, ttok)
                P.I("pool", "tensor_tensor", [ttok, btok2], [utok], out=U[:, j, 1:L - 1], in0=tbuf[:, 1:L - 1], in1=bbuf[:, 1:L - 1], op=ALU.mult)

    def down(hidx):
        c0, c1, o0, o1 = halves[hidx]
        osbs = subblocks(o0, o1, 344)
        ksplit = [(0, KC2)] if KC2 <= 11 else [(0, 11), (11, KC2 - 11)]
        for jo in range(8):
            ws = [(kb.load_w(W2, k0, kn, jo * 128, 128), k0, kn) for (k0, kn) in ksplit]
            for (s0, s1) in osbs:
                bk, bt = kb.bank()
                pairs = []
                rd = [utok]
                for ((w, wt), k0, kn) in ws:
                    rd.append(wt)
                    for kc in range(kn):
                        pairs.append((w[:, kc, :], U[:, k0 + kc, s0 - c0:s1 - c0]))
                kb.mm_acc(bk[:, 0:s1 - s0], bt, pairs, rd)
                P.I("dve", "scalar_tensor_tensor", [bt, gtok, xtok], [xtok], out=xT[:, jo, s0:s1], in0=bk[:, 0:s1 - s0], scalar=gate[:, jo:jo + 1], in1=xT[:, jo, s0:s1], op0=ALU.mult, op1=ALU.add)

    def norm(hidx):
        c0, c1, _, _ = halves[hidx]
        norm_mod(kb, xT, [xtok], c0, c1, nwv, nbv, nvtoks, hT, [htok], hoff=0, maxw=344)

    norm(0)
    up(0)
    norm(1)
    down(0)
    up(1)
    down(1)


def build_s3():
    nc = bass.Bass("TRN2", target_bir_lowering=False)
    with ExitStack() as st:
        kb = KB(nc, st)
        P = kb.P
        x_ext = kb.dram_in("x_ext", [NT, D])
        ag = kb.dram_in("ag", [128, 8, NT], BF16)
        mvec_d = kb.dram_in("mvec", [128, 2])
        ident_d = kb.dram_in("ident", [128, 128])
        c_fm = kb.dram_in("c_fm", [128, 8])
        ada_b_d = [kb.dram_in("ada_b%d" % l, [128, 48]) for l in range(2)]
        nw_d = kb.dram_in("nw", [128, 4, 8])
        fcw_d = [kb.dram_in("fcw%d" % l, [128, 22, 3]) for l in range(2)]
        fcb_d = [kb.dram_in("fcb%d" % l, [128, 22]) for l in range(2)]
        ocw_d = kb.dram_in("ocw", [128, 8, 3])
        ocb_d = kb.dram_in("ocb", [128, 8])
        ada_w = [kb.dram_in("ada_w%d" % l, [D, 6 * D]) for l in range(2)]
        ev_w_out = kb.dram_in("ev_w_out", [D, D])
        w_up = [kb.dram_in("w_up%d" % l, [D, 2 * DFF]) for l in range(2)]
        w_down = [kb.dram_in("w_down%d" % l, [DFF, D]) for l in range(2)]
        od_w_in = kb.dram_in("od_w_in", [D, 3 * D])
        od_w_out = kb.dram_in("od_w_out", [D, D])
        out_d = kb.dram_out("out", [SEG, D])

        xT = kb.sb("xT", [128, 8, NT], F32)
        xtok = Tok()
        hT = kb.sb("hT", [128, 8, 1032], BF16)
        htok = Tok()
        big = kb.sb("big", [128, 22 * 1032], BF16)
        bigtok = Tok()
        kb.gc_a = [kb.sb("gca%d" % i, [128, 1032], F32) for i in range(2)]
        kb.gc_a_tok = [Tok() for _ in range(2)]
        kb.gc_t = [kb.sb("gct%d" % i, [128, 1032], F32) for i in range(2)]
        kb.gc_t_tok = [Tok() for _ in range(2)]
        kb.gc_b = kb.sb("gcb", [128, 1032], F32)
        kb.gc_b_tok = Tok()
        mvec, mvtok = kb.load_small("mvec_sb", mvec_d[:, :], [128, 2])
        ident, itok = kb.load_small("ident_sb", ident_d[:, :], [128, 128])
        cT, ctok = kb.load_small("cT", c_fm[:, :], [128, 8])
        ada_b = [kb.load_small("adab%d" % l, ada_b_d[l][:, :], [128, 48]) for l in range(2)]
        nw, nwtok = kb.load_small("nw_sb", nw_d[:, :, :], [128, 4, 8])
        fcw = [kb.load_small("fcw_sb%d" % l, fcw_d[l][:, :, :], [128, 22, 3]) for l in range(2)]
        fcb = [kb.load_small("fcb_sb%d" % l, fcb_d[l][:, :], [128, 22]) for l in range(2)]
        ocw, ocwtok = kb.load_small("ocw_sb", ocw_d[:, :, :], [128, 8, 3])
        ocb, ocbtok = kb.load_small("ocb_sb", ocb_d[:, :], [128, 8])

        scT = kb.sb("scT", [128, 8, 1], F32)
        sctok = Tok()
        P.I("act", "activation", [ctok], [sctok], out=scT[:, :, 0], in_=cT[:, :], func=AF.Silu)

        load_xT(kb, x_ext, NT, xT, xtok, ident, itok, kb.gc_a, kb.gc_a_tok)

        mod = [kb.sb("mod%d" % l, [128, 1, 48], F32) for l in range(2)]
        modtok = [Tok(), Tok()]
        compute_mod(kb, ada_w[0], ada_b[0][0], ada_b[0][1], scT, sctok, 1, list(range(16, 48)), mod[0], modtok[0])
        compute_mod(kb, ada_w[1], ada_b[1][0], ada_b[1][1], scT, sctok, 1, list(range(0, 48)), mod[1], modtok[1])
        weff = kb.sb("weff", [128, 3, 8], F32)
        wefftok = Tok()
        for i, (l, c0) in enumerate([(0, 32), (1, 8), (1, 32)]):
            P.I("dve", "scalar_tensor_tensor", [modtok[l], nwtok], [wefftok], out=weff[:, i, :], in0=mod[l][:, 0, c0:c0 + 8], scalar=1.0, in1=nw[:, i, :], op0=ALU.add, op1=ALU.mult)

        for (c0, c1) in [(0, 1028), (1028, NT)]:
            L = c1 - c0
            agsb = big[:, 0:8 * L].rearrange("p (k n) -> p k n", k=8)
            P.dma(agsb[:, 0:4, :], ag[:, 0:4, c0:c1], writes=[bigtok])
            P.dma(agsb[:, 4:8, :], ag[:, 4:8, c0:c1], writes=[bigtok])
            for jo in range(8):
                w, wt = kb.load_w(ev_w_out, 0, 8, jo * 128, 128)
                for (s0, s1) in subblocks(c0, c1, 344):
                    bk, bt = kb.bank()
                    kb.mm_acc(bk[:, 0:s1 - s0], bt, [(w[:, kc, :], agsb[:, kc, s0 - c0:s1 - c0]) for kc in range(8)], [wt, bigtok])
                    P.I("dve", "scalar_tensor_tensor", [bt, modtok[0], xtok], [xtok], out=xT[:, jo, s0:s1], in0=bk[:, 0:s1 - s0], scalar=mod[0][:, 0, 16 + jo:17 + jo], in1=xT[:, jo, s0:s1], op0=ALU.mult, op1=ALU.add)

        U22 = big[:, 0:22 * 1032].rearrange("p (k n) -> p k n", k=22)
        U8 = big[:, 0:8 * 1032].rearrange("p (k n) -> p k n", k=8)
        glu_conv_block(kb, "ffn", weff[:, 0, :], mod[0][:, 0, 24:32], [wefftok, modtok[0]], w_up[0], w_down[0], 22, fcw[0][0], fcb[0][0], fcw[0][1],
                       mod[0][:, 0, 40:48], modtok[0], xT, xtok, mvec, mvtok, hT, htok, U22, bigtok)
        glu_conv_block(kb, "sc", weff[:, 1, :], mod[1][:, 0, 0:8], [wefftok, modtok[1]], od_w_in, od_w_out, 8, ocw, ocb, ocwtok,
                       mod[1][:, 0, 16:24], modtok[1], xT, xtok, mvec, mvtok, hT, htok, U8, bigtok)
        glu_conv_block(kb, "ffn", weff[:, 2, :], mod[1][:, 0, 24:32], [wefftok, modtok[1]], w_up[1], w_down[1], 22, fcw[1][0], fcb[1][0], fcw[1][1],
                       mod[1][:, 0, 40:48], modtok[1], xT, xtok, mvec, mvtok, hT, htok, U22, bigtok)

        yT = kb.sb("yT", [128, 8, 128], F32)
        ytok = Tok()
        otile = kb.gc_a
        ottok = kb.gc_a_tok
        for ti in range(16):
            c0 = HALO + ti * 128
            norm_mod(kb, xT, [xtok], c0, c0 + 128, nw[:, 3, :], None, [nwtok], yT, [ytok])
            b = ti % 2
            for half in range(2):
                bk, bt = kb.bank()
                for q in range(4):
                    kc = half * 4 + q
                    P.I("pe", "transpose", [ytok, itok], [bt], inc=(q == 3), out=bk[:, q * 128:(q + 1) * 128], in_=yT[:, kc, :], identity=ident[:, :])
                if half == 0:
                    P.I("act", "copy", [bt], [ottok[b]], out=otile[b][:, 0:512], in_=bk[:, :])
                else:
                    P.I("dve", "tensor_copy", [bt], [ottok[b]], out=otile[b][:, 512:1024], in_=bk[:, :])
            P.dma(out_d[ti * 128:(ti + 1) * 128, :], otile[b][:, 0:1024], reads=[ottok[b]])
        P.finish("sp")
        P.emit()
    return nc


def fm(v, nch):
    return np.ascontiguousarray(np.asarray(v, np.float32).reshape(nch, 128).T)


def s3_inputs(core, inp, AG):
    b, j = core // 4, core % 4
    T0 = j * SEG
    x = inp["x"]
    x_ext = np.zeros((NT, D), np.float32)
    agx = np.zeros((NT, D), np.float32)
    lo, hi = max(0, T0 - HALO), min(SEQ, T0 + SEG + HALO)
    x_ext[lo - (T0 - HALO):hi - (T0 - HALO)] = x[b, lo:hi]
    agx[lo - (T0 - HALO):hi - (T0 - HALO)] = AG[b, lo:hi]
    mv = np.array([0.0 if T0 == 0 else 1.0, 0.0 if T0 + SEG == SEQ else 1.0], np.float32)
    ag = np.ascontiguousarray(agx.reshape(NT, 8, 128).transpose(2, 1, 0)).astype(ml_dtypes.bfloat16)
    m = {
        "x_ext": x_ext, "ag": ag,
        "mvec": np.ascontiguousarray(np.broadcast_to(mv[None, :], (128, 2))),
        "ident": np.eye(128, dtype=np.float32),
        "c_fm": fm(inp["c"][b], 8),
        "ada_b0": fm(inp["ada_b"][0], 48), "ada_b1": fm(inp["ada_b"][1], 48),
        "nw": np.ascontiguousarray(np.stack([fm(inp["norm_ffn_w"][0], 8), fm(inp["norm_mix_w"][1], 8), fm(inp["norm_ffn_w"][1], 8), fm(inp["final_norm_w"], 8)], axis=1)),
        "ocw": np.ascontiguousarray(np.stack([fm(inp["od_conv_w"][0][t], 8) for t in range(3)], axis=2)),
        "ocb": fm(inp["od_conv_b"][0], 8),
        "ev_w_out": np.asarray(inp["ev_w_out"][0]), "od_w_in": np.asarray(inp["od_w_in"][0]), "od_w_out": np.asarray(inp["od_w_out"][0]),
    }
    for l in range(2):
        m["fcw%d" % l] = np.ascontiguousarray(np.stack([fm(inp["ffn_conv_w"][l][t], 22) for t in range(3)], axis=2))
        m["fcb%d" % l] = fm(inp["ffn_conv_b"][l], 22)
        m["ada_w%d" % l] = np.asarray(inp["ada_w"][l])
        m["w_up%d" % l] = np.asarray(inp["ffn_w_up"][l])
        m["w_down%d" % l] = np.asarray(inp["ffn_w_down"][l])
    return m


NS1 = 2560
OWN0 = 256
NEG = -30000.0


class Carver:
    def __init__(self, scr):
        self.scr = scr
        self.off = 0

    def take(self, nelem, dt):
        if dt == BF16:
            n32 = (nelem + 1) // 2
            v = self.scr[:, self.off:self.off + n32].bitcast(BF16)[:, 0:nelem]
        else:
            n32 = nelem
            v = self.scr[:, self.off:self.off + n32]
        self.off += n32
        return v

    def reset(self):
        self.off = 0


def build_s1(do_na=True):
    nc = bass.Bass("TRN2", target_bir_lowering=False)
    with ExitStack() as st:
        kb = KB(nc, st)
        P = kb.P
        x_s1 = kb.dram_in("x_s1", [NS1, D])
        ctx_d = kb.dram_in("ctx", [256, D])
        c2_d = kb.dram_in("c2", [128, 8, 2])
        ident_d = kb.dram_in("ident", [128, 128])
        ada_b_d = kb.dram_in("ada_b0", [128, 48])
        nw_d = kb.dram_in("nw0", [128, 8])
        ada_w = kb.dram_in("ada_w0", [D, 6 * D])
        w_in = kb.dram_in("ev_w_in", [D, 4 * D])
        biasG_d = kb.dram_in("biasG", [128, 5 * 8 * 128])
        biasE_d = kb.dram_in("biasE", [4, 128, 5 * 8 * 128])
        out_a = kb.dram_out("out_a", [128, 4, SEG], BF16)
        out_hg = kb.dram_out("out_hg", [20, 128, SEG])
        out_chg = kb.dram_out("out_chg", [12, 128, 256])

        QT = kb.sb("QT", [128, 4, SEG], BF16)
        qtok = Tok()
        KT = kb.sb("KT", [128, 4, NS1], BF16)
        ktok = Tok()
        KcT = kb.sb("KcT", [128, 4, 256], BF16)
        kctok = Tok()
        V = kb.sb("V", [128, 22, 512], BF16)
        vtok = Tok()
        SCR = 24576
        scr = kb.sb("scr", [128, SCR], F32)
        cv = Carver(scr)

        ident, itok = kb.load_small("ident_sb", ident_d[:, :], [128, 128])
        c2, c2tok = kb.load_small("c2_sb", c2_d[:, :, :], [128, 8, 2])
        ada_b, abtok = kb.load_small("adab", ada_b_d[:, :], [128, 48])
        nw, nwtok = kb.load_small("nw_sb", nw_d[:, :], [128, 8])
        onesb = kb.sb("onesb", [128, 128], BF16)
        onesb_tok = Tok()
        P.I("pool", "memset", [], [onesb_tok], ap=onesb[:], constant=1.0)

        scT = kb.sb("scT", [128, 8, 2], F32)
        sctok = Tok()
        P.I("act", "activation", [c2tok], [sctok], out=scT[:, :, :], in_=c2[:, :, :], func=AF.Silu)
        mod = kb.sb("mod", [128, 2, 48], F32)
        modtok = Tok()
        compute_mod(kb, ada_w, ada_b, abtok, scT, sctok, 2, list(range(0, 16)), mod, modtok)
        weff = kb.sb("weff", [128, 2, 8], F32)
        wefftok = Tok()
        for r in range(2):
            P.I("dve", "scalar_tensor_tensor", [modtok, nwtok], [wefftok], out=weff[:, r, :], in0=mod[:, r, 8:16], scalar=1.0, in1=nw[:, :], op0=ALU.add, op1=ALU.mult)

        hT = cv.take(8 * NS1, BF16).rearrange("p (k n) -> p k n", k=8)
        htok = Tok()
        hcT = cv.take(8 * 256, BF16).rearrange("p (k n) -> p k n", k=8)
        hctok = Tok()
        xTb = cv.take(8 * 512, F32).rearrange("p (k n) -> p k n", k=8)
        xbtok = Tok()
        Wv = cv.take(8 * 512, BF16).rearrange("p (k n) -> p k n", k=8)
        wvtok = Tok()
        xt = [cv.take(1024, F32) for _ in range(2)]
        xtt = [Tok(), Tok()]
        stg = [cv.take(512, F32) for _ in range(3)]
        stgt = [Tok() for _ in range(3)]

        for blk in range(5):
            load_xT(kb, x_s1[blk * 512:(blk + 1) * 512, :], 512, xTb, xbtok, ident, itok, xt, xtt)
            norm_mod(kb, xTb, [xbtok], 0, 512, weff[:, 0, :], mod[:, 0, 0:8], [wefftok, modtok], hT, [htok], hoff=blk * 512)
        load_xT(kb, ctx_d, 256, xTb, xbtok, ident, itok, xt, xtt)
        norm_mod(kb, xTb, [xbtok], 0, 256, weff[:, 1, :], mod[:, 1, 0:8], [wefftok, modtok], hcT, [hctok], hoff=0)

        ev = [0]

        def evac(dst, src, rd, wr):
            ev[0] += 1
            if ev[0] % 2 == 0:
                P.I("act", "copy", rd, wr, out=dst, in_=src)
            else:
                P.I("dve", "tensor_copy", rd, wr, out=dst, in_=src)

        own_sbs = [(OWN0 + i * 512, OWN0 + (i + 1) * 512) for i in range(4)]
        all_sbs = [(i * 512, (i + 1) * 512) for i in range(5)]
        for n in range(32):
            grp = n // 4
            if grp == 2:
                continue
            w, wt = kb.load_w(w_in, 0, 8, n * 128, 128)
            sbs = all_sbs if grp == 1 else own_sbs
            for (s0, s1) in sbs:
                bk, bt = kb.bank()
                kb.mm_acc(bk[:, 0:512], bt, [(w[:, kc, :], hT[:, kc, s0:s1]) for kc in range(8)], [wt, htok])
                if grp == 0:
                    evac(QT[:, n, s0 - OWN0:s1 - OWN0], bk[:, 0:512], [bt], [qtok])
                elif grp == 1:
                    evac(KT[:, n - 4, s0:s1], bk[:, 0:512], [bt], [ktok])
                else:
                    i = kb.rot("stg", 3)
                    evac(stg[i][:, 0:512], bk[:, 0:512], [bt], [stgt[i]])
                    P.dma(out_hg[n - 12, :, s0 - OWN0:s1 - OWN0], stg[i][:, 0:512], reads=[stgt[i]])
            if grp == 1 or grp in (4, 5, 6):
                bk, bt = kb.bank()
                kb.mm_acc(bk[:, 0:256], bt, [(w[:, kc, :], hcT[:, kc, :]) for kc in range(8)], [wt, hctok])
                if grp == 1:
                    evac(KcT[:, n - 4, :], bk[:, 0:256], [bt], [kctok])
                else:
                    i = kb.rot("stg", 3)
                    evac(stg[i][:, 0:256], bk[:, 0:256], [bt], [stgt[i]])
                    P.dma(out_chg[n - 16, :, :], stg[i][:, 0:256], reads=[stgt[i]])
        for c in range(4):
            i = kb.rot("wst", len(kb.wst))
            stgw = kb.wst[i][:, 0:1024].rearrange("p (kc n) -> p kc n", kc=8)
            P.dma(stgw, w_in[:, 1024 + c * 128:1024 + (c + 1) * 128].rearrange("(kc p) n -> p kc n", p=128), writes=[kb.wst_tok[i]])
            P.I("pool", "tensor_copy", [kb.wst_tok[i]], [wvtok], out=Wv[:, :, c * 128:(c + 1) * 128], in_=stgw)
        for s in range(22):
            bk, bt = kb.bank()
            if s < 20:
                pairs = [(hT[:, kc, s * 128:(s + 1) * 128], Wv[:, kc, :]) for kc in range(8)]
                rd = [htok, wvtok]
            else:
                pairs = [(hcT[:, kc, (s - 20) * 128:(s - 19) * 128], Wv[:, kc, :]) for kc in range(8)]
                rd = [hctok, wvtok]
            kb.mm_acc(bk[:, 0:512], bt, pairs, rd)
            evac(V[:, s, :], bk[:, 0:512], [bt], [vtok])

        P.barrier()
        cv.reset()
        biasG = cv.take(5120, F32).rearrange("p (c h q) -> p c h q", c=5, h=8)
        bgtok = Tok()
        biasE = cv.take(5120, F32)
        betok = Tok()
        sst = [cv.take(512, F32) for _ in range(4)]
        sstt = [Tok() for _ in range(4)]
        PT = [cv.take(7 * 1024, BF16).rearrange("p (c n) -> p c n", c=7) for _ in range(2)]
        pttok = [Tok(), Tok()]
        rD = [cv.take(512, F32) for _ in range(2)]
        rdtok = [Tok(), Tok()]
        AT = cv.take(4 * SEG, BF16).rearrange("p (k n) -> p k n", k=4)
        attok = Tok()
        assert cv.off <= SCR
        P.dma(biasG.rearrange("p c h q -> p (c h q)"), biasG_d[:, :], writes=[bgtok])

        S_banks = [(kb.banks[i], kb.btok[i]) for i in range(4)]
        O_banks = [(kb.banks[4], kb.btok[4]), (kb.banks[5], kb.btok[5])]
        D_banks = [(kb.banks[6], kb.btok[6]), (kb.banks[7], kb.btok[7])]
        sri = [0]
        edge_of = {0: 0, 1: 1, 14: 2, 15: 3}

        def qk(t):
            pb = t % 2
            if t in edge_of:
                P.dma(biasE, biasE_d[edge_of[t], :, :], writes=[betok])
                btab = biasE.rearrange("p (c h q) -> p c h q", c=5, h=8)
                btk = betok
            else:
                btab, btk = biasG, bgtok
            for ci in range(7):
                for half in range(2):
                    bk, bt = S_banks[sri[0]]
                    sri[0] = (sri[0] + 1) % 4
                    for hh in range(4):
                        h = 2 * hh + half
                        pr = half * 64
                        if ci < 5:
                            lhsT = KT[pr:pr + 64, h // 2, (t + ci) * 128:(t + ci + 1) * 128]
                            rd = [ktok, qtok]
                        else:
                            lhsT = KcT[pr:pr + 64, h // 2, (ci - 5) * 128:(ci - 4) * 128]
                            rd = [kctok, qtok]
                        P.I("pe", "matmul", rd, [bt], inc=(hh == 3), out=bk[:, hh * 128:(hh + 1) * 128], lhsT=lhsT,
                            rhs=QT[pr:pr + 64, h // 2, t * 128:(t + 1) * 128], start=True, stop=True)
                    dst = PT[pb][:, ci, half * 512:(half + 1) * 512]
                    if ci < 5:
                        si = kb.rot("sst", 4)
                        P.I("dve", "scalar_tensor_tensor", [bt, btk], [sstt[si]], out=sst[si].rearrange("p (h q) -> p h q", h=4),
                            in0=bk[:, :].rearrange("p (h q) -> p h q", h=4), scalar=0.125, in1=btab[:, ci, half * 4:half * 4 + 4, :], op0=ALU.mult, op1=ALU.add)
                        P.I("act", "activation", [sstt[si]], [pttok[pb]], out=dst, in_=sst[si], func=AF.Exp)
                    else:
                        P.I("act", "activation", [bt], [pttok[pb]], out=dst, in_=bk[:, :], func=AF.Exp, scale=0.125)

        def pv(t):
            pb = t % 2
            for half in range(2):
                ob, ot = O_banks[half]
                for hh in range(4):
                    for ci in range(7):
                        vt = (t + ci) if ci < 5 else (20 + ci - 5)
                        P.I("pe", "matmul", [vtok, pttok[pb]], [ot], inc=(ci == 6), out=ob[:, hh * 128:(hh + 1) * 128],
                            lhsT=V[:, vt, hh * 128:(hh + 1) * 128], rhs=PT[pb][:, ci, half * 512 + hh * 128:half * 512 + (hh + 1) * 128], start=(ci == 0), stop=(ci == 6))
                db, dt_ = D_banks[half]
                for ci in range(7):
                    P.I("pe", "matmul", [onesb_tok, pttok[pb]], [dt_], inc=(ci == 6), out=db[:, :], lhsT=onesb[:, :],
                        rhs=PT[pb][:, ci, half * 512:(half + 1) * 512], start=(ci == 0), stop=(ci == 6))
                P.I("dve", "reciprocal", [dt_], [rdtok[half]], out=rD[half], in_=db[:, :])
                pr = half * 64
                ov = ob[pr:pr + 64, :].rearrange("p (j q) -> p j q", j=4)
                rv = rD[half][pr:pr + 64, :].rearrange("p (j q) -> p j q", j=4)
                P.I("dve", "tensor_tensor", [ot, rdtok[half]], [attok], out=AT[pr:pr + 64, :, t * 128:(t + 1) * 128], in0=ov, in1=rv, op=ALU.mult)

        if do_na:
            qk(0)
        for t in range(16 if do_na else 0):
            if t + 1 < 16:
                qk(t + 1)
            pv(t)
        if not do_na:
            P.I("pool", "memset", [], [attok], ap=AT[:, :, :], constant=0.0)
        for k in range(4):
            P.dma(out_a[:, k, :], AT[:, k, :], reads=[attok])
        P.finish("sp")
        P.emit()
    return nc


def _slot_rows(j, s):
    R0 = j * 32
    r = R0 - 4 + 2 * s
    if r < 0:
        return (6, 7) if s == 0 else None
    if r >= 128:
        return (120, 121) if s == 18 else None
    return (r, r + 1)


def _bias_table(j, t, rpb):
    R0 = j * 32
    tab = np.full((2, 64, 5, 8, 2, 64), NEG, np.float32)
    c = np.arange(64)
    cs = np.clip(c - 8, 0, 48)
    kc = np.arange(64)
    colvalid = (kc[:, None] >= cs[None, :]) & (kc[:, None] < cs[None, :] + 16)
    colrel = np.clip(kc[:, None] - c[None, :] + 15, 0, 30)
    used = [set(), set()]
    for p in range(5):
        rows = _slot_rows(j, t + p)
        if rows is None:
            continue
        for kpar in range(2):
            kr = rows[kpar]
            for qpar in range(2):
                qr = R0 + 2 * t + qpar
                r0 = min(max(qr - 4, 0), 120)
                if not (r0 <= kr < r0 + 8) or kr in used[qpar]:
                    continue
                used[qpar].add(kr)
                rr = kr - qr + 7
                vals = rpb[:, rr, :][:, colrel]
                vals = np.where(colvalid[None], vals, NEG)
                tab[kpar, :, p, :, qpar, :] = vals.transpose(1, 0, 2)
    assert all(len(u) == 8 for u in used), (j, t, used)
    tab = tab[:, :, :, [0, 2, 4, 6, 1, 3, 5, 7]]
    return np.ascontiguousarray(tab).reshape(128, 5 * 8 * 128)


def s1_inputs(core, inp):
    b, j = core // 4, core % 4
    x = inp["x"]
    x_s1 = np.zeros((NS1, D), np.float32)
    for s in range(20):
        rows = _slot_rows(j, s)
        if rows is not None:
            x_s1[s * 128:(s + 1) * 128] = x[b, rows[0] * 64:rows[0] * 64 + 128]
    rpb = np.asarray(inp["na_rpb"][0], np.float32)
    c2 = np.ascontiguousarray(np.stack([fm(inp["c"][b], 8), fm(inp["c_ctx"], 8)], axis=2))
    return {
        "x_s1": x_s1, "ctx": np.ascontiguousarray(inp["ctx"][b]), "c2": c2,
        "ident": np.eye(128, dtype=np.float32),
        "ada_b0": fm(inp["ada_b"][0], 48), "nw0": fm(inp["norm_mix_w"][0], 8),
        "ada_w0": np.asarray(inp["ada_w"][0]), "ev_w_in": np.asarray(inp["ev_w_in"][0]),
        "biasG": _bias_table(1, 5, rpb),
        "biasE": np.ascontiguousarray(np.stack([_bias_table(j, t, rpb) for t in (0, 1, 14, 15)], axis=0)),
    }


def build_s2():
    nc = bass.Bass("TRN2", target_bir_lowering=False)
    with ExitStack() as st:
        kb = KB(nc, st, n_wst=1, n_wbf=1)
        P = kb.P
        qT_d = kb.dram_in("qT", [128, SEQ])
        fT_d = kb.dram_in("fT", [2, 128, SEQ])
        ftok_d = kb.dram_in("ftok", [2, SEQ, 128])
        itok_d = kb.dram_in("itok", [SEQ, 128])
        gT_d = kb.dram_in("gT", [128, SEQ])
        cftok_d = kb.dram_in("cftok", [2, 256, 128])
        citok_d = kb.dram_in("citok", [256, 128])
        lbT_d = kb.dram_in("lbT_in", [128, 2, 2])
        lbrow_d = kb.dram_in("lbrow_in", [128, 2, 2, 128])
        normw_d = kb.dram_in("normw", [128, 1])
        masks_d = kb.dram_in("masks", [128, 4, 128])
        ci_d = kb.dram_in("ci", [128, 4])
        yT_d = kb.dram_out("yT", [128, SEQ], BF16)

        masks, mtok = kb.load_small("masks_sb", masks_d[:, :, :], [128, 4, 128])
        CI, citok = kb.load_small("ci_sb", ci_d[:, :], [128, 4])
        lbl, lbltok = kb.load_small("lbl_sb", lbT_d[:, :, :], [128, 2, 2])
        lbr, lbrtok = kb.load_small("lbr_sb", lbrow_d[:, :, :, :], [128, 2, 2, 128])
        normw, nwtok = kb.load_small("normw_sb", normw_d[:, :], [128, 1])

        lbT = kb.sb("lbT", [128, 2], F32)
        omlT = kb.sb("omlT", [128, 2], F32)
        lbrow = kb.sb("lbrow", [128, 2, 128], F32)
        omlrow = kb.sb("omlrow", [128, 2, 128], F32)
        lbtok = Tok()
        P.I("dve", "tensor_tensor", [lbltok], [lbtok], out=lbT[:, :], in0=lbl[:, :, 0], in1=lbl[:, :, 1], op=ALU.subtract)
        P.I("act", "activation", [lbtok], [lbtok], out=lbT[:, :], in_=lbT[:, :], func=AF.Sigmoid)
        P.I("dve", "tensor_scalar", [lbtok], [lbtok], out=omlT[:, :], in0=lbT[:, :], scalar1=-1.0, scalar2=1.0, op0=ALU.mult, op1=ALU.add)
        P.I("dve", "tensor_tensor", [lbrtok], [lbtok], out=lbrow[:, :, :], in0=lbr[:, :, 0, :], in1=lbr[:, :, 1, :], op=ALU.subtract)
        P.I("act", "activation", [lbtok], [lbtok], out=lbrow[:, :, :], in_=lbrow[:, :, :], func=AF.Sigmoid)
        P.I("dve", "tensor_scalar", [lbtok], [lbtok], out=omlrow[:, :, :], in0=lbrow[:, :, :], scalar1=-1.0, scalar2=1.0, op0=ALU.mult, op1=ALU.add)

        obwd = kb.sb("obwd", [128, SEQ], F32)
        obtok = Tok()
        S = kb.sb("S", [128, 128], F32)
        stok = Tok()
        NSB = 6
        Sb = [kb.sb("Sb%d" % i, [128, 128], BF16) for i in range(NSB)]
        sbtok = [Tok() for _ in range(NSB)]
        ybuf = [kb.sb("ybuf%d" % i, [128, 512], BF16) for i in range(2)]
        ybtok = [Tok(), Tok()]

        tmps = {}

        def tmp(name, width=128, dt=F32, n=2):
            if name not in tmps:
                tmps[name] = ([kb.sb("%s_%d" % (name, i), [128, width], dt) for i in range(n)], [Tok() for _ in range(n)])
            bufs, toks = tmps[name]
            i = kb.rot("tmp_" + name, n)
            return bufs[i], toks[i]

        grp_bufs = {}
        for nm, shape in (("q", [128, 512]), ("fT", [128, 512]), ("ft", [128, 4, 128]), ("it", [128, 4, 128]), ("g", [128, 512])):
            grp_bufs[nm] = ([kb.sb("grp_%s%d" % (nm, i), shape, F32) for i in range(2)], [Tok(), Tok()])

        def load_group(g, d, full, ctx):
            out = {}
            i = kb.rot("grp", 2)
            if ctx:
                b, t = grp_bufs["ft"][0][i], grp_bufs["ft"][1][i]
                P.dma(b[:, 0:2, :], cftok_d[d].rearrange("(n p) k -> p n k", p=128), writes=[t])
                out["ft"] = (b, t)
                b, t = grp_bufs["it"][0][i], grp_bufs["it"][1][i]
                P.dma(b[:, 0:2, :], citok_d.rearrange("(n p) k -> p n k", p=128), writes=[t])
                out["it"] = (b, t)
                return out
            c0 = g * 512
            b, t = grp_bufs["ft"][0][i], grp_bufs["ft"][1][i]
            P.dma(b[:, :, :], ftok_d[d, c0:c0 + 512, :].rearrange("(n p) k -> p n k", p=128), writes=[t])
            out["ft"] = (b, t)
            b, t = grp_bufs["it"][0][i], grp_bufs["it"][1][i]
            P.dma(b[:, :, :], itok_d[c0:c0 + 512, :].rearrange("(n p) k -> p n k", p=128), writes=[t])
            out["it"] = (b, t)
            b, t = grp_bufs["q"][0][i], grp_bufs["q"][1][i]
            P.dma(b[:, :], qT_d[:, c0:c0 + 512], writes=[t])
            out["q"] = (b, t)
            b, t = grp_bufs["fT"][0][i], grp_bufs["fT"][1][i]
            P.dma(b[:, :], fT_d[d, :, c0:c0 + 512], writes=[t])
            out["fT"] = (b, t)
            if d == 0:
                b, t = grp_bufs["g"][0][i], grp_bufs["g"][1][i]
                P.dma(b[:, :], gT_d[:, c0:c0 + 512], writes=[t])
                out["g"] = (b, t)
            return out

        sb_cur = [0]

        def tile(d, gi, ti, full, tok0):
            Mi = masks[:, 2 * d, :]
            Mx = masks[:, 2 * d + 1, :]
            ftb, ftt = gi["ft"]
            itb, itt = gi["it"]
            sg, sgt = tmp("sg")
            P.I("act", "activation", [ftt], [sgt], out=sg[:, :], in_=ftb[:, ti, :], func=AF.Sigmoid)
            f, ft_ = tmp("f")
            P.I("dve", "tensor_tensor", [sgt, lbtok], [ft_], out=f[:, :], in0=sg[:, :], in1=omlrow[:, d, :], op=ALU.mult)
            P.I("dve", "tensor_tensor", [ft_, lbtok], [ft_], out=f[:, :], in0=f[:, :], in1=lbrow[:, d, :], op=ALU.add)
            lf, lft = tmp("lf")
            P.I("act", "activation", [ft_], [lft], out=lf[:, :], in_=f[:, :], func=AF.Ln)
            kt, ktt = tmp("kt")
            P.I("pool", "tensor_scalar", [ft_], [ktt], out=kt[:, :], in0=f[:, :], scalar1=-1.0, scalar2=1.0, op0=ALU.mult, op1=ALU.add)
            vb, vbt = tmp("vb", 128, BF16)
            P.I("pool", "tensor_copy", [itt], [vbt], out=vb[:, :], in_=itb[:, ti, :])
            bk, bt = kb.bank()
            P.I("pe", "matmul", [mtok, lft], [bt], out=bk[:, 0:128], lhsT=Mx, rhs=lf[:, :], start=True, stop=True)
            eD, eDt = tmp("eD")
            P.I("act", "activation", [bt], [eDt], out=eD[:, :], in_=bk[:, 0:128], func=AF.Exp)
            P.I("dve", "tensor_tensor", [eDt, ktt], [eDt], out=eD[:, :], in0=eD[:, :], in1=kt[:, :], op=ALU.mult)
            kdm, kdmt = tmp("kdm", 512, BF16)
            for c in range(4):
                P.I("pool", "tensor_scalar", [eDt, citok], [kdmt], out=kdm[:, c * 128:(c + 1) * 128], in0=eD[:, :], scalar1=CI[:, c:c + 1], scalar2=None, op0=ALU.mult)
            bk2, bt2 = kb.bank()
            P.I("pe", "matmul", [citok, lft], [bt2], out=bk2[:, 0:4], lhsT=lf[:, :], rhs=CI[:, :], start=True, stop=True)
            ebt, ebtt = tmp("ebt", 4)
            P.I("act", "activation", [bt2], [ebtt], out=ebt[:, :], in_=bk2[:, 0:4], func=AF.Exp)
            bkS, btS = kb.bank()
            for c in range(4):
                P.I("pe", "matmul", [kdmt, vbt], [btS], inc=(c == 3), out=bkS[:, c * 128:(c + 1) * 128], lhsT=kdm[:, c * 128:(c + 1) * 128], rhs=vb[:, :], start=True, stop=True)
            corder = [0, 1, 2, 3] if d == 0 else [3, 2, 1, 0]
            if full:
                qb, qt_ = gi["q"]
                fTb, fTt = gi["fT"]
                sgT, sgTt = tmp("sgT")
                P.I("act", "activation", [fTt], [sgTt], out=sgT[:, :], in_=fTb[:, ti * 128:(ti + 1) * 128], func=AF.Sigmoid)
                P.I("dve", "tensor_scalar", [sgTt, lbtok], [sgTt], out=sgT[:, :], in0=sgT[:, :], scalar1=omlT[:, d:d + 1], scalar2=lbT[:, d:d + 1], op0=ALU.mult, op1=ALU.add)
                kT, kTt = tmp("kT")
                P.I("pool", "tensor_scalar", [sgTt], [kTt], out=kT[:, :], in0=sgT[:, :], scalar1=-1.0, scalar2=1.0, op0=ALU.mult, op1=ALU.add)
                bk3, bt3 = kb.bank()
                P.I("pe", "matmul", [mtok, lft], [bt3], out=bk3[:, 0:128], lhsT=lf[:, :], rhs=Mi, start=True, stop=True)
                eB, eBt = tmp("eB")
                P.I("act", "activation", [bt3], [eBt], out=eB[:, :], in_=bk3[:, 0:128], func=AF.Exp)
                enB, enBt = tmp("enB")
                P.I("act", "activation", [bt3], [enBt], out=enB[:, :], in_=bk3[:, 0:128], func=AF.Exp, scale=-1.0)
                qs, qst = tmp("qs")
                P.I("act", "activation", [qt_], [qst], out=qs[:, :], in_=qb[:, ti * 128:(ti + 1) * 128], func=AF.Silu)
                qe, qet = tmp("qe", 128, BF16)
                P.I("dve", "tensor_tensor", [qst, eBt], [qet], out=qe[:, :], in0=qs[:, :], in1=eB[:, :], op=ALU.mult)
                ke, ket = tmp("ke", 128, BF16)
                P.I("pool", "tensor_tensor", [kTt, enBt], [ket], out=ke[:, :], in0=kT[:, :], in1=enB[:, :], op=ALU.mult)
                bk4, bt4 = kb.bank()
                P.I("pe", "matmul", [ket, qet], [bt4], out=bk4[:, 0:128], lhsT=ke[:, :], rhs=qe[:, :], start=True, stop=True)
                am, amt = tmp("am", 128, BF16)
                P.I("dve", "tensor_tensor", [bt4, mtok], [amt], out=am[:, :], in0=bk4[:, 0:128], in1=Mi, op=ALU.mult)
                bkO, btO = kb.bank()
                P.I("pe", "matmul", [vbt, amt], [btO], inc=False, out=bkO[:, 0:128], lhsT=vb[:, :], rhs=am[:, :], start=True, stop=False, skip_group_check=True)
            for n_, c in enumerate(corder):
                if full:
                    cur = sb_cur[0]
                    P.I("pe", "matmul", [sbtok[cur], qet], [btO], inc=(n_ == 3), out=bkO[:, c * 32:(c + 1) * 32],
                        lhsT=Sb[cur][:, :], rhs=qe[:, c * 32:(c + 1) * 32], start=False, stop=(n_ == 3), skip_group_check=True)
                P.I("dve", "scalar_tensor_tensor", [btS, ebtt, stok], [stok], out=S[:, :], in0=S[:, :], scalar=ebt[:, c:c + 1], in1=bkS[:, c * 128:(c + 1) * 128], op0=ALU.mult, op1=ALU.add)
                if full or (n_ == 3):
                    nxt = (sb_cur[0] + 1) % NSB
                    P.I("pool", "tensor_copy", [stok], [sbtok[nxt]], out=Sb[nxt][:, :], in_=S[:, :])
                    sb_cur[0] = nxt
            if not full:
                return
            if d == 1:
                P.I("act", "copy", [btO], [obtok], out=obwd[:, tok0:tok0 + 128], in_=bkO[:, 0:128])
                return
            gb, gt_ = gi["g"]
            o, ot = tmp("o")
            P.I("dve", "tensor_tensor", [btO, obtok], [ot], out=o[:, :], in0=bkO[:, 0:128], in1=obwd[:, tok0:tok0 + 128], op=ALU.add)
            sq, sqt = tmp("sq")
            P.I("act", "activation", [ot], [sqt], out=sq[:, :], in_=o[:, :], func=AF.Square)
            bk5, bt5 = kb.bank()
            P.I("pe", "matmul", [sqt, kb.ones_tok], [bt5], out=bk5[:, 0:128], lhsT=kb.ones32[:], rhs=sq[:, :], start=True, stop=True)
            rs, rst = tmp("rs")
            P.I("act", "activation", [bt5, kb.eps_tok], [rst], out=rs[:, :], in_=bk5[:, 0:128], func=AF.Sqrt, bias=kb.eps_sb[:, 0:1], scale=1.0 / 128)
            P.I("dve", "reciprocal", [rst], [rst], out=rs[:, :], in_=rs[:, :])
            P.I("dve", "scalar_tensor_tensor", [ot, rst, nwtok], [ot], out=o[:, :], in0=o[:, :], scalar=normw[:, 0:1], in1=rs[:, :], op0=ALU.mult, op1=ALU.mult)
            sl, slt = tmp("sl")
            P.I("act", "activation", [gt_], [slt], out=sl[:, :], in_=gb[:, ti * 128:(ti + 1) * 128], func=AF.Silu)
            yb = (tok0 // 512) % 2
            P.I("pool", "tensor_tensor", [ot, slt], [ybtok[yb]], out=ybuf[yb][:, ti * 128:(ti + 1) * 128], in0=o[:, :], in1=sl[:, :], op=ALU.mult)
            if ti == 3:
                c0 = (tok0 // 512) * 512
                P.dma(yT_d[:, c0:c0 + 512], ybuf[yb][:, :], reads=[ybtok[yb]])

        for d in (1, 0):
            P.I("pool", "memset", [stok], [stok], ap=S[:, :], constant=0.0)
            gi = load_group(0, d, False, True)
            for ti in ([0, 1] if d == 0 else [1, 0]):
                tile(d, gi, ti, False, 0)
            groups = list(range(16)) if d == 0 else list(range(15, -1, -1))
            for g in groups:
                gi = load_group(g, d, True, False)
                for ti in ([0, 1, 2, 3] if d == 0 else [3, 2, 1, 0]):
                    tile(d, gi, ti, True, g * 512 + ti * 128)
        P.finish("sp")
        P.emit()
    return nc


def _hg_masks():
    s = np.arange(128)[:, None]
    t = np.arange(128)[None, :]
    same = (s // 32) == (t // 32)
    Mi_f = (same & (s <= t)).astype(np.float32)
    Mx_f = (same & (s > t)).astype(np.float32)
    m = np.stack([Mi_f, Mx_f, Mi_f.T, Mx_f.T], axis=1)
    ci = (np.arange(128)[:, None] // 32 == np.arange(4)[None, :]).astype(np.float32)
    return np.ascontiguousarray(m), np.ascontiguousarray(ci)


def s2_inputs(core, inp, hg_full, chg_full):
    b, hd = core // 4, core % 4
    sl = lambda grp: slice(grp * 512 + hd * 128, grp * 512 + (hd + 1) * 128)
    q = hg_full[b][:, sl(0)]
    ff = hg_full[b][:, sl(1)]
    fb = hg_full[b][:, sl(2)]
    iv = hg_full[b][:, sl(3)]
    g = hg_full[b][:, sl(4)]
    lbl = np.asarray(inp["hg_lb_logits"], np.float32)[:, :, hd * 128:(hd + 1) * 128]
    masks, ci = _hg_masks()
    return {
        "qT": np.ascontiguousarray(q.T), "fT": np.ascontiguousarray(np.stack([ff.T, fb.T], axis=0)),
        "ftok": np.ascontiguousarray(np.stack([ff, fb], axis=0)), "itok": np.ascontiguousarray(iv),
        "gT": np.ascontiguousarray(g.T),
        "cftok": np.ascontiguousarray(np.stack([chg_full[b][:, hd * 128:(hd + 1) * 128], chg_full[b][:, 512 + hd * 128:512 + (hd + 1) * 128]], axis=0)),
        "citok": np.ascontiguousarray(chg_full[b][:, 1024 + hd * 128:1024 + (hd + 1) * 128]),
        "lbT_in": np.ascontiguousarray(lbl.transpose(2, 1, 0)),
        "lbrow_in": np.ascontiguousarray(np.broadcast_to(lbl.transpose(1, 0, 2)[None], (128, 2, 2, 128))),
        "normw": np.ascontiguousarray(np.asarray(inp["hg_norm_w"][0], np.float32)[hd * 128:(hd + 1) * 128].reshape(128, 1)),
        "masks": masks, "ci": ci,
    }


def kernel(**inputs):
    inp = {k: np.asarray(v) for k, v in inputs.items()}
    cores = list(range(NCORES))
    nc1 = build_s1()
    r1 = run_bass_kernel_spmd(nc1, [s1_inputs(c, inp) for c in cores], core_ids=cores).results
    hg_full = np.zeros((2, SEQ, 2560), np.float32)
    chg_full = np.zeros((2, 256, 1536), np.float32)
    AG = np.zeros((2, SEQ, D), np.float32)
    for c in cores:
        b, j = c // 4, c % 4
        hg_full[b, j * SEG:(j + 1) * SEG] = np.asarray(r1[c]["out_hg"]).transpose(2, 0, 1).reshape(SEG, 2560)
        if j == 0:
            chg_full[b] = np.asarray(r1[c]["out_chg"]).transpose(2, 0, 1).reshape(256, 1536)
        AG[b, j * SEG:(j + 1) * SEG, 0:512] = np.asarray(r1[c]["out_a"]).transpose(2, 1, 0).reshape(SEG, 512)
    nc2 = build_s2()
    r2 = run_bass_kernel_spmd(nc2, [s2_inputs(c, inp, hg_full, chg_full) for c in cores], core_ids=cores).results
    for c in cores:
        b, hd = c // 4, c % 4
        AG[b, :, 512 + hd * 128:512 + (hd + 1) * 128] = np.asarray(r2[c]["yT"]).T
    nc3 = build_s3()
    r3 = run_bass_kernel_spmd(nc3, [s3_inputs(c, inp, AG) for c in cores], core_ids=cores).results
    out = np.zeros((2, SEQ, D), np.float32)
    for c in cores:
        b, j = c // 4, c % 4
        out[b, j * SEG:(j + 1) * SEG] = np.asarray(r3[c]["out"])
    return out
```

```python
from contextlib import ExitStack
import numpy as np
import ml_dtypes
import concourse.bass as bass
import concourse.mybir as mybir
from concourse.bass_utils import run_bass_kernel_spmd

F32 = mybir.dt.float32
BF16 = mybir.dt.bfloat16
AF = mybir.ActivationFunctionType
ALU = mybir.AluOpType

SAME_ENGINE_SYNC = 0
SEM_GEN = 30000
EPS = 1e-6
NCORES = 8
D = 1024
SEQ = 8192
SEG = 2048
HALO = 4
NT = SEG + 2 * HALO
DFF = 2816


class Tok:
    __slots__ = ("w", "r")

    def __init__(self):
        self.w = None
        self.r = []


class Prog:
    ENGS = ("pe", "act", "dve", "pool", "sp")

    def __init__(self, nc, stack):
        self.nc = nc
        self.stack = stack
        self.lists = {e: [] for e in self.ENGS}
        self.cur_sem = {}
        self.cnt = {}
        self.seen = {e: {} for e in self.ENGS}
        self.pending = {e: False for e in self.ENGS}
        self.nsem = 0
        for e in ("pe", "act", "dve", "pool"):
            self._new_sem(e)
        self.dma_pool = {"sp": [[self._alloc_sem(), 0] for _ in range(16)]}
        self.dma_rr = {"sp": 0}
        self.n_ins = 0

    def _alloc_sem(self):
        self.nsem += 1
        return self.stack.enter_context(self.nc.semaphore("s%d" % self.nsem))

    def _new_sem(self, e):
        self.cur_sem[e] = self._alloc_sem()
        self.cnt[e] = 0

    def _need(self, e, ev, raw=True):
        if ev is None:
            return
        feng, sem, val = ev
        if feng == e and (e == "pe" or SAME_ENGINE_SYNC == 0 or (SAME_ENGINE_SYNC == 1 and not raw)):
            return
        k = id(sem)
        if self.seen[e].get(k, 0) >= val:
            return
        self.seen[e][k] = val
        self.lists[e].append(("wait", sem, val))

    def _deps(self, e, reads, writes):
        for t in reads:
            self._need(e, t.w)
        for t in writes:
            self._need(e, t.w, raw=False)
            for ev in t.r:
                self._need(e, ev, raw=False)

    def _record(self, ev, reads, writes):
        for t in reads:
            t.r.append(ev)
            if len(t.r) > 64:
                t.r = t.r[-48:]
        for t in writes:
            t.w = ev
            t.r = []

    def op(self, e, fn, reads=(), writes=(), inc=True):
        assert inc or e == "pe"
        self._deps(e, reads, writes)
        if inc and self.cnt[e] >= SEM_GEN and not self.pending[e]:
            self._new_sem(e)
        sem = self.cur_sem[e]
        val = self.cnt[e] + 1
        if inc:
            self.cnt[e] = val
            self.pending[e] = False
        else:
            self.pending[e] = True
        ev = (e, sem, val)
        self.lists[e].append(("ins", fn, sem if inc else None, 1))
        self._record(ev, reads, writes)
        self.n_ins += 1
        return ev

    def I(self, e, meth, reads=(), writes=(), inc=True, **kw):
        def fn(eng, meth=meth, kw=kw):
            return getattr(eng, meth)(**kw)
        return self.op(e, fn, reads=reads, writes=writes, inc=inc)

    def dma(self, out, in_, reads=(), writes=(), q="sp"):
        pool = self.dma_pool[q]
        i = self.dma_rr[q]
        self.dma_rr[q] = (i + 1) % len(pool)
        slot = pool[i]
        sem = slot[0]
        if slot[1] > 0:
            self._need(q, ("dma", sem, slot[1]))
        self._deps(q, reads, writes)
        slot[1] += 16
        ev = ("dma", sem, slot[1])

        def fn(eng, out=out, in_=in_):
            return eng.dma_start(out=out, in_=in_)

        self.lists[q].append(("ins", fn, sem, 16))
        self._record(ev, reads, writes)
        self.n_ins += 1
        return ev

    def barrier(self):
        evs = []
        for x in ("pe", "act", "dve", "pool"):
            assert not self.pending[x], x
            if self.cnt[x] > 0:
                evs.append((x, self.cur_sem[x], self.cnt[x]))
        for q, pool in self.dma_pool.items():
            for sem, v in pool:
                if v > 0:
                    evs.append(("dma", sem, v))
        for e in self.ENGS:
            for ev in evs:
                if ev[0] == e and e != "dma":
                    if e == "pe" or SAME_ENGINE_SYNC == 0:
                        continue
                self._need(e, ev)

    def finish(self, e="sp"):
        for x in ("pe", "act", "dve", "pool"):
            assert not self.pending[x], x
            if self.cnt[x] > 0:
                self._need(e, (x, self.cur_sem[x], self.cnt[x]))
        for q, pool in self.dma_pool.items():
            for sem, v in pool:
                if v > 0:
                    self._need(e, ("dma", sem, v))

    def emit(self):
        nc = self.nc
        lists = self.lists
        with nc.Block() as block:
            def run(eng, lst):
                for it in lst:
                    if it[0] == "wait":
                        eng.wait_ge(it[1], it[2])
                    else:
                        ins = it[1](eng)
                        if it[2] is not None:
                            if it[3] is None:
                                ins.then_inc(it[2])
                            else:
                                ins.then_inc(it[2], it[3])

            @block.tensor
            def _(eng):
                run(eng, lists["pe"])

            @block.scalar
            def _(eng):
                run(eng, lists["act"])

            @block.vector
            def _(eng):
                run(eng, lists["dve"])

            @block.gpsimd
            def _(eng):
                run(eng, lists["pool"])

            @block.sync
            def _(eng):
                run(eng, lists["sp"])


WELEMS = 1408


class KB:
    def __init__(self, nc, st, n_wst=3, n_wbf=4):
        self.nc = nc
        self.st = st
        self.P = Prog(nc, st)
        self.banks = [st.enter_context(nc.psum_tensor("bank%d" % i, [128, 512], F32)) for i in range(8)]
        self.btok = [Tok() for _ in range(8)]
        self.bi = 0
        self.nring = 8
        self.ring0 = 0
        self.wst = [self.sb("wst%d" % i, [128, WELEMS], F32) for i in range(n_wst)]
        self.wst_tok = [Tok() for _ in range(n_wst)]
        self.wbf = [self.sb("wbf%d" % i, [128, WELEMS], BF16) for i in range(n_wbf)]
        self.wbf_tok = [Tok() for _ in range(n_wbf)]
        self.rr = {}
        self.ones32 = self.sb("ones32", [128, 128], F32)
        self.ones_tok = Tok()
        self.P.I("pool", "memset", [], [self.ones_tok], ap=self.ones32[:], constant=1.0)
        self.eps_sb = self.sb("eps_sb", [128, 1], F32)
        self.eps_tok = Tok()
        self.P.I("pool", "memset", [], [self.eps_tok], ap=self.eps_sb[:], constant=EPS)

    def sb(self, name, shape, dt):
        return self.st.enter_context(self.nc.sbuf_tensor(name, shape, dt))

    def dram_in(self, name, shape, dt=F32):
        return self.nc.dram_tensor(name, list(shape), dt, kind="ExternalInput").ap()

    def dram_out(self, name, shape, dt=F32):
        return self.nc.dram_tensor(name, list(shape), dt, kind="ExternalOutput").ap()

    def set_ring(self, start, n):
        self.ring0 = start
        self.nring = n
        self.bi = 0

    def bank(self):
        i = self.ring0 + self.bi
        self.bi = (self.bi + 1) % self.nring
        return self.banks[i], self.btok[i]

    def rot(self, key, n):
        i = self.rr.get(key, 0)
        self.rr[key] = (i + 1) % n
        return i

    def load_small(self, name, ap_dram, shape, dt=F32):
        t = self.sb(name, shape, dt)
        tk = Tok()
        self.P.dma(t[:], ap_dram, writes=[tk])
        return t, tk

    def load_w(self, W, k0, KC, c0, n=128, cast=True):
        P = self.P
        assert KC * n <= WELEMS
        i = self.rot("wst", len(self.wst))
        stg_flat = self.wst[i][:, 0:KC * n]
        stg = stg_flat.rearrange("p (kc n) -> p kc n", kc=KC)
        src = W[k0 * 128:(k0 + KC) * 128, c0:c0 + n].rearrange("(kc p) n -> p kc n", p=128)
        P.dma(stg, src, writes=[self.wst_tok[i]])
        if not cast:
            return stg, self.wst_tok[i]
        j = self.rot("wbf", len(self.wbf))
        dst = self.wbf[j][:, 0:KC * n]
        P.I("act", "copy", [self.wst_tok[i]], [self.wbf_tok[j]], out=dst, in_=stg_flat)
        return dst.rearrange("p (kc n) -> p kc n", kc=KC), self.wbf_tok[j]

    def mm_acc(self, ps_ap, ptok, pairs, reads):
        n = len(pairs)
        for i, (l, r) in enumerate(pairs):
            self.P.I("pe", "matmul", reads, [ptok], inc=(i == n - 1), out=ps_ap, lhsT=l, rhs=r, start=(i == 0), stop=(i == n - 1))


def subblocks(c0, c1, maxw=512):
    L = c1 - c0
    n = (L + maxw - 1) // maxw
    base = L // n
    rem = L % n
    out = []
    s = c0
    for i in range(n):
        w = base + (1 if i < rem else 0)
        out.append((s, s + w))
        s += w
    return out


def compute_mod(kb, ada_w, ada_b_sb, ada_b_tok, scT, scT_tok, nrhs, chunks, out, out_tok):
    P = kb.P
    for n in chunks:
        w, wt = kb.load_w(ada_w, 0, 8, n * 128, 128, cast=False)
        bk, bt = kb.bank()
        kb.mm_acc(bk[:, 0:nrhs], bt, [(w[:, kc, :], scT[:, kc, :]) for kc in range(8)], [wt, scT_tok])
        for r in range(nrhs):
            P.I("dve", "tensor_tensor", [bt, ada_b_tok], [out_tok], out=out[:, r, n:n + 1], in0=bk[:, r:r + 1], in1=ada_b_sb[:, n:n + 1], op=ALU.add)


def load_xT(kb, x_dram, ntok, xT, xtok, ident, ident_tok, xt, xtt):
    P = kb.P
    nt = (ntok + 127) // 128
    for i in range(nt):
        r0 = i * 128
        rows = min(128, ntok - r0)
        b = i % 2
        P.dma(xt[b][0:rows, 0:1024], x_dram[r0:r0 + rows, :], writes=[xtt[b]])
        for half in range(2):
            bk, bt = kb.bank()
            for q in range(4):
                kc = half * 4 + q
                P.I("pe", "transpose", [xtt[b], ident_tok], [bt], inc=(q == 3), out=bk[:, q * 128:q * 128 + rows],
                    in_=xt[b][0:rows, kc * 128:(kc + 1) * 128], identity=ident[0:rows, 0:rows])
            src = bk[:, :].rearrange("p (q n) -> p q n", q=4)[:, :, 0:rows]
            dst = xT[:, half * 4:half * 4 + 4, r0:r0 + rows]
            if half == 0:
                P.I("act", "copy", [bt], [xtok], out=dst, in_=src)
            else:
                P.I("dve", "tensor_copy", [bt], [xtok], out=dst, in_=src)


def norm_mod(kb, xT, xtoks, c0, c1, wv, bv, vtoks, hT, htoks, hoff=0, maxw=512):
    P = kb.P
    if not hasattr(kb, "nm_sq"):
        kb.nm_sq = [kb.sb("nmsq%d" % i, [128, 512], F32) for i in range(2)]
        kb.nm_sq_tok = [Tok() for _ in range(2)]
        kb.nm_r = [kb.sb("nmr%d" % i, [128, 512], F32) for i in range(2)]
        kb.nm_r_tok = [Tok() for _ in range(2)]
        kb.nm_t = [kb.sb("nmt%d" % i, [128, 512], F32) for i in range(2)]
        kb.nm_t_tok = [Tok() for _ in range(2)]
    for (s0, s1) in subblocks(c0, c1, maxw):
        w = s1 - s0
        bk, bt = kb.bank()
        for kc in range(8):
            i = kb.rot("nmsq", 2)
            sq = kb.nm_sq[i]
            P.I("act", "activation", xtoks, [kb.nm_sq_tok[i]], out=sq[:, 0:w], in_=xT[:, kc, s0:s1], func=AF.Square)
            P.I("pe", "matmul", [kb.nm_sq_tok[i], kb.ones_tok], [bt], out=bk[:, 0:w], lhsT=kb.ones32[:], rhs=sq[:, 0:w], start=(kc == 0), stop=(kc == 7))
        ri = kb.rot("nmr", 2)
        r = kb.nm_r[ri]
        rt = kb.nm_r_tok[ri]
        P.I("act", "activation", [bt, kb.eps_tok], [rt], out=r[:, 0:w], in_=bk[:, 0:w], func=AF.Sqrt, bias=kb.eps_sb[:, 0:1], scale=1.0 / D)
        P.I("dve", "reciprocal", [rt], [rt], out=r[:, 0:w], in_=r[:, 0:w])
        for kc in range(8):
            dst = hT[:, kc, hoff + s0 - c0:hoff + s1 - c0]
            if bv is None:
                P.I("dve", "scalar_tensor_tensor", list(xtoks) + [rt] + list(vtoks), htoks, out=dst, in0=xT[:, kc, s0:s1], scalar=wv[:, kc:kc + 1], in1=r[:, 0:w], op0=ALU.mult, op1=ALU.mult)
            else:
                ti = kb.rot("nmt", 2)
                t = kb.nm_t[ti]
                tt = kb.nm_t_tok[ti]
                P.I("dve", "scalar_tensor_tensor", list(xtoks) + [rt] + list(vtoks), [tt], out=t[:, 0:w], in0=xT[:, kc, s0:s1], scalar=wv[:, kc:kc + 1], in1=r[:, 0:w], op0=ALU.mult, op1=ALU.mult)
                P.I("act", "activation", [tt] + list(vtoks), htoks, out=dst, in_=t[:, 0:w], func=AF.Identity, bias=bv[:, kc:kc + 1], scale=1.0)


def glu_conv_block(kb, kind, nwv, nbv, nvtoks, W1, W2, KC2, cw, cb, cvtok, gate, gtok, xT, xtok, mvec, mvtok, hT, htok, U, utok):
    P = kb.P
    halves = [(0, 1030, 1, 1028), (1026, NT, 1028, NT - 1)]

    def up(hidx):
        c0, c1, o0, o1 = halves[hidx]
        L = c1 - c0
        sbs = subblocks(c0, c1, 344)
        for j in range(KC2):
            ai = kb.rot("gca", 2)
            abuf, atok = kb.gc_a[ai], kb.gc_a_tok[ai]
            ti = kb.rot("gct", 2)
            tbuf, ttok = kb.gc_t[ti], kb.gc_t_tok[ti]
            bbuf, btok2 = kb.gc_b, kb.gc_b_tok

            def proj(col):
                w, wt = kb.load_w(W1, 0, 8, col * 128, 128)
                outs = []
                for (s0, s1) in sbs:
                    bk, bt = kb.bank()
                    kb.mm_acc(bk[:, 0:s1 - s0], bt, [(w[:, kc, :], hT[:, kc, s0 - c0:s1 - c0]) for kc in range(8)], [wt, htok])
                    outs.append((bk, bt, s0, s1))
                return outs

            def edge_mask(buf, tok):
                if hidx == 0:
                    P.I("pool", "tensor_scalar", [tok, mvtok], [tok], out=buf[:, 0:HALO], in0=buf[:, 0:HALO], scalar1=mvec[:, 0:1], scalar2=1.0, op0=ALU.mult, op1=ALU.mult)
                else:
                    P.I("pool", "tensor_scalar", [tok, mvtok], [tok], out=buf[:, L - HALO:L], in0=buf[:, L - HALO:L], scalar1=mvec[:, 1:2], scalar2=1.0, op0=ALU.mult, op1=ALU.mult)

            def conv(src, srct, dst, dstt):
                P.I("act", "activation", [srct, cvtok], [dstt], out=dst[:, 1:L - 1], in_=src[:, 1:L - 1], func=AF.Identity, bias=cb[:, j:j + 1], scale=cw[:, j, 1:2])
                P.I("dve", "scalar_tensor_tensor", [srct, cvtok], [dstt], out=dst[:, 1:L - 1], in0=src[:, 0:L - 2], scalar=cw[:, j, 0:1], in1=dst[:, 1:L - 1], op0=ALU.mult, op1=ALU.add)
                P.I("dve", "scalar_tensor_tensor", [srct, cvtok], [dstt], out=dst[:, 1:L - 1], in0=src[:, 2:L], scalar=cw[:, j, 2:3], in1=dst[:, 1:L - 1], op0=ALU.mult, op1=ALU.add)

            if kind == "ffn":
                for (bk, bt, s0, s1) in proj(j):
                    P.I("act", "copy", [bt], [atok], out=abuf[:, s0 - c0:s1 - c0], in_=bk[:, 0:s1 - s0])
                edge_mask(abuf, atok)
                conv(abuf, atok, tbuf, ttok)
                P.I("act", "activation", [ttok], [ttok], out=tbuf[:, 1:L - 1], in_=tbuf[:, 1:L - 1], func=AF.Gelu)
                for (bk, bt, s0, s1) in proj(KC2 + j):
                    a0 = max(s0, c0 + 1)
                    a1 = min(s1, c1 - 1)
                    P.I("dve", "tensor_tensor", [bt, ttok], [utok], out=U[:, j, a0 - c0:a1 - c0], in0=tbuf[:, a0 - c0:a1 - c0], in1=bk[:, a0 - s0:a1 - s0], op=ALU.mult)
            else:
                for (bk, bt, s0, s1) in proj(j):
                    P.I("act", "copy", [bt], [btok2], out=bbuf[:, s0 - c0:s1 - c0], in_=bk[:, 0:s1 - s0])
                for (bk, bt, s0, s1) in proj(KC2 + j):
                    P.I("act", "copy", [bt], [ttok], out=tbuf[:, s0 - c0:s1 - c0], in_=bk[:, 0:s1 - s0])
                edge_mask(tbuf, ttok)
                for (bk, bt, s0, s1) in proj(2 * KC2 + j):
                    P.I("dve", "tensor_tensor", [bt, ttok], [atok], out=abuf[:, s0 - c0:s1 - c0], in0=tbuf[:, s0 - c0:s1 - c0], in1=bk[:, 0:s1 - s0], op=ALU.mult)
                conv(abuf, atok, tbuf, ttok)
                P.I("pool", "tensor_tensor", [ttok, btok2], [utok], out=U[:, j, 1:L - 1], in0=tbuf[:, 1:L - 1], in1=bbuf[:, 1:L - 1], op=ALU.mult)

    def down(hidx):
        c0, c1, o0, o1 = halves[hidx]
        osbs = subblocks(o0, o1, 344)
        ksplit = [(0, KC2)] if KC2 <= 11 else [(0, 11), (11, KC2 - 11)]
        for jo in range(8):
            ws = [(kb.load_w(W2, k0, kn, jo * 128, 128), k0, kn) for (k0, kn) in ksplit]
            for (s0, s1) in osbs:
                bk, bt = kb.bank()
                pairs = []
                rd = [utok]
                for ((w, wt), k0, kn) in ws:
                    rd.append(wt)
                    for kc in range(kn):
                        pairs.append((w[:, kc, :], U[:, k0 + kc, s0 - c0:s1 - c0]))
                kb.mm_acc(bk[:, 0:s1 - s0], bt, pairs, rd)
                P.I("dve", "scalar_tensor_tensor", [bt, gtok, xtok], [xtok], out=xT[:, jo, s0:s1], in0=bk[:, 0:s1 - s0], scalar=gate[:, jo:jo + 1], in1=xT[:, jo, s0:s1], op0=ALU.mult, op1=ALU.add)

    def norm(hidx):
        c0, c1, _, _ = halves[hidx]
        norm_mod(kb, xT, [xtok], c0, c1, nwv, nbv, nvtoks, hT, [htok], hoff=0, maxw=344)

    norm(0)
    up(0)
    norm(1)
    down(0)
    up(1)
    down(1)


def build_s3():
    nc = bass.Bass("TRN2", target_bir_lowering=False)
    with ExitStack() as st:
        kb = KB(nc, st)
        P = kb.P
        x_ext = kb.dram_in("x_ext", [NT, D])
        ag = kb.dram_in("ag", [128, 8, NT], BF16)
        mvec_d = kb.dram_in("mvec", [128, 2])
        ident_d = kb.dram_in("ident", [128, 128])
        c_fm = kb.dram_in("c_fm", [128, 8])
        ada_b_d = [kb.dram_in("ada_b%d" % l, [128, 48]) for l in range(2)]
        nw_d = kb.dram_in("nw", [128, 4, 8])
        fcw_d = [kb.dram_in("fcw%d" % l, [128, 22, 3]) for l in range(2)]
        fcb_d = [kb.dram_in("fcb%d" % l, [128, 22]) for l in range(2)]
        ocw_d = kb.dram_in("ocw", [128, 8, 3])
        ocb_d = kb.dram_in("ocb", [128, 8])
        ada_w = [kb.dram_in("ada_w%d" % l, [D, 6 * D]) for l in range(2)]
        ev_w_out = kb.dram_in("ev_w_out", [D, D])
        w_up = [kb.dram_in("w_up%d" % l, [D, 2 * DFF]) for l in range(2)]
        w_down = [kb.dram_in("w_down%d" % l, [DFF, D]) for l in range(2)]
        od_w_in = kb.dram_in("od_w_in", [D, 3 * D])
        od_w_out = kb.dram_in("od_w_out", [D, D])
        out_d = kb.dram_out("out", [SEG, D])

        xT = kb.sb("xT", [128, 8, NT], F32)
        xtok = Tok()
        hT = kb.sb("hT", [128, 8, 1032], BF16)
        htok = Tok()
        big = kb.sb("big", [128, 22 * 1032], BF16)
        bigtok = Tok()
        kb.gc_a = [kb.sb("gca%d" % i, [128, 1032], F32) for i in range(2)]
        kb.gc_a_tok = [Tok() for _ in range(2)]
        kb.gc_t = [kb.sb("gct%d" % i, [128, 1032], F32) for i in range(2)]
        kb.gc_t_tok = [Tok() for _ in range(2)]
        kb.gc_b = kb.sb("gcb", [128, 1032], F32)
        kb.gc_b_tok = Tok()
        mvec, mvtok = kb.load_small("mvec_sb", mvec_d[:, :], [128, 2])
        ident, itok = kb.load_small("ident_sb", ident_d[:, :], [128, 128])
        cT, ctok = kb.load_small("cT", c_fm[:, :], [128, 8])
        ada_b = [kb.load_small("adab%d" % l, ada_b_d[l][:, :], [128, 48]) for l in range(2)]
        nw, nwtok = kb.load_small("nw_sb", nw_d[:, :, :], [128, 4, 8])
        fcw = [kb.load_small("fcw_sb%d" % l, fcw_d[l][:, :, :], [128, 22, 3]) for l in range(2)]
        fcb = [kb.load_small("fcb_sb%d" % l, fcb_d[l][:, :], [128, 22]) for l in range(2)]
        ocw, ocwtok = kb.load_small("ocw_sb", ocw_d[:, :, :], [128, 8, 3])
        ocb, ocbtok = kb.load_small("ocb_sb", ocb_d[:, :], [128, 8])

        scT = kb.sb("scT", [128, 8, 1], F32)
        sctok = Tok()
        P.I("act", "activation", [ctok], [sctok], out=scT[:, :, 0], in_=cT[:, :], func=AF.Silu)

        load_xT(kb, x_ext, NT, xT, xtok, ident, itok, kb.gc_a, kb.gc_a_tok)

        mod = [kb.sb("mod%d" % l, [128, 1, 48], F32) for l in range(2)]
        modtok = [Tok(), Tok()]
        compute_mod(kb, ada_w[0], ada_b[0][0], ada_b[0][1], scT, sctok, 1, list(range(16, 48)), mod[0], modtok[0])
        compute_mod(kb, ada_w[1], ada_b[1][0], ada_b[1][1], scT, sctok, 1, list(range(0, 48)), mod[1], modtok[1])
        weff = kb.sb("weff", [128, 3, 8], F32)
        wefftok = Tok()
        for i, (l, c0) in enumerate([(0, 32), (1, 8), (1, 32)]):
            P.I("dve", "scalar_tensor_tensor", [modtok[l], nwtok], [wefftok], out=weff[:, i, :], in0=mod[l][:, 0, c0:c0 + 8], scalar=1.0, in1=nw[:, i, :], op0=ALU.add, op1=ALU.mult)

        for (c0, c1) in [(0, 1028), (1028, NT)]:
            L = c1 - c0
            agsb = big[:, 0:8 * L].rearrange("p (k n) -> p k n", k=8)
            P.dma(agsb[:, 0:4, :], ag[:, 0:4, c0:c1], writes=[bigtok])
            P.dma(agsb[:, 4:8, :], ag[:, 4:8, c0:c1], writes=[bigtok])
            for jo in range(8):
                w, wt = kb.load_w(ev_w_out, 0, 8, jo * 128, 128)
                for (s0, s1) in subblocks(c0, c1, 344):
                    bk, bt = kb.bank()
                    kb.mm_acc(bk[:, 0:s1 - s0], bt, [(w[:, kc, :], agsb[:, kc, s0 - c0:s1 - c0]) for kc in range(8)], [wt, bigtok])
                    P.I("dve", "scalar_tensor_tensor", [bt, modtok[0], xtok], [xtok], out=xT[:, jo, s0:s1], in0=bk[:, 0:s1 - s0], scalar=mod[0][:, 0, 16 + jo:17 + jo], in1=xT[:, jo, s0:s1], op0=ALU.mult, op1=ALU.add)

        U22 = big[:, 0:22 * 1032].rearrange("p (k n) -> p k n", k=22)
        U8 = big[:, 0:8 * 1032].rearrange("p (k n) -> p k n", k=8)
        glu_conv_block(kb, "ffn", weff[:, 0, :], mod[0][:, 0, 24:32], [wefftok, modtok[0]], w_up[0], w_down[0], 22, fcw[0][0], fcb[0][0], fcw[0][1],
                       mod[0][:, 0, 40:48], modtok[0], xT, xtok, mvec, mvtok, hT, htok, U22, bigtok)
        glu_conv_block(kb, "sc", weff[:, 1, :], mod[1][:, 0, 0:8], [wefftok, modtok[1]], od_w_in, od_w_out, 8, ocw, ocb, ocwtok,
                       mod[1][:, 0, 16:24], modtok[1], xT, xtok, mvec, mvtok, hT, htok, U8, bigtok)
        glu_conv_block(kb, "ffn", weff[:, 2, :], mod[1][:, 0, 24:32], [wefftok, modtok[1]], w_up[1], w_down[1], 22, fcw[1][0], fcb[1][0], fcw[1][1],
                       mod[1][:, 0, 40:48], modtok[1], xT, xtok, mvec, mvtok, hT, htok, U22, bigtok)

        yT = kb.sb("yT", [128, 8, 128], F32)
        ytok = Tok()
        otile = kb.gc_a
        ottok = kb.gc_a_tok
        for ti in range(16):
            c0 = HALO + ti * 128
            norm_mod(kb, xT, [xtok], c0, c0 + 128, nw[:, 3, :], None, [nwtok], yT, [ytok])
            b = ti % 2
            for half in range(2):
                bk, bt = kb.bank()
                for q in range(4):
                    kc = half * 4 + q
                    P.I("pe", "transpose", [ytok, itok], [bt], inc=(q == 3), out=bk[:, q * 128:(q + 1) * 128], in_=yT[:, kc, :], identity=ident[:, :])
                if half == 0:
                    P.I("act", "copy", [bt], [ottok[b]], out=otile[b][:, 0:512], in_=bk[:, :])
                else:
                    P.I("dve", "tensor_copy", [bt], [ottok[b]], out=otile[b][:, 512:1024], in_=bk[:, :])
            P.dma(out_d[ti * 128:(ti + 1) * 128, :], otile[b][:, 0:1024], reads=[ottok[b]])
        P.finish("sp")
        P.emit()
    return nc


def fm(v, nch):
    return np.ascontiguousarray(np.asarray(v, np.float32).reshape(nch, 128).T)


def s3_inputs(core, inp, AG):
    b, j = core // 4, core % 4
    T0 = j * SEG
    x = inp["x"]
    x_ext = np.zeros((NT, D), np.float32)
    agx = np.zeros((NT, D), np.float32)
    lo, hi = max(0, T0 - HALO), min(SEQ, T0 + SEG + HALO)
    x_ext[lo - (T0 - HALO):hi - (T0 - HALO)] = x[b, lo:hi]
    agx[lo - (T0 - HALO):hi - (T0 - HALO)] = AG[b, lo:hi]
    mv = np.array([0.0 if T0 == 0 else 1.0, 0.0 if T0 + SEG == SEQ else 1.0], np.float32)
    ag = np.ascontiguousarray(agx.reshape(NT, 8, 128).transpose(2, 1, 0)).astype(ml_dtypes.bfloat16)
    m = {
        "x_ext": x_ext, "ag": ag,
        "mvec": np.ascontiguousarray(np.broadcast_to(mv[None, :], (128, 2))),
        "ident": np.eye(128, dtype=np.float32),
        "c_fm": fm(inp["c"][b], 8),
        "ada_b0": fm(inp["ada_b"][0], 48), "ada_b1": fm(inp["ada_b"][1], 48),
        "nw": np.ascontiguousarray(np.stack([fm(inp["norm_ffn_w"][0], 8), fm(inp["norm_mix_w"][1], 8), fm(inp["norm_ffn_w"][1], 8), fm(inp["final_norm_w"], 8)], axis=1)),
        "ocw": np.ascontiguousarray(np.stack([fm(inp["od_conv_w"][0][t], 8) for t in range(3)], axis=2)),
        "ocb": fm(inp["od_conv_b"][0], 8),
        "ev_w_out": np.asarray(inp["ev_w_out"][0]), "od_w_in": np.asarray(inp["od_w_in"][0]), "od_w_out": np.asarray(inp["od_w_out"][0]),
    }
    for l in range(2):
        m["fcw%d" % l] = np.ascontiguousarray(np.stack([fm(inp["ffn_conv_w"][l][t], 22) for t in range(3)], axis=2))
        m["fcb%d" % l] = fm(inp["ffn_conv_b"][l], 22)
        m["ada_w%d" % l] = np.asarray(inp["ada_w"][l])
        m["w_up%d" % l] = np.asarray(inp["ffn_w_up"][l])
        m["w_down%d" % l] = np.asarray(inp["ffn_w_down"][l])
    return m


NS1 = 2560
OWN0 = 256
NEG = -30000.0


class Carver:
    def __init__(self, scr):
        self.scr = scr
        self.off = 0

    def take(self, nelem, dt):
        if dt == BF16:
            n32 = (nelem + 1) // 2
            v = self.scr[:, self.off:self.off + n32].bitcast(BF16)[:, 0:nelem]
        else:
            n32 = nelem
            v = self.scr[:, self.off:self.off + n32]
        self.off += n32
        return v

    def reset(self):
        self.off = 0


def build_s1(do_na=True):
    nc = bass.Bass("TRN2", target_bir_lowering=False)
    with ExitStack() as st:
        kb = KB(nc, st)
        P = kb.P
        x_s1 = kb.dram_in("x_s1", [NS1, D])
        ctx_d = kb.dram_in("ctx", [256, D])
        c2_d = kb.dram_in("c2", [128, 8, 2])
        ident_d = kb.dram_in("ident", [128, 128])
        ada_b_d = kb.dram_in("ada_b0", [128, 48])
        nw_d = kb.dram_in("nw0", [128, 8])
        ada_w = kb.dram_in("ada_w0", [D, 6 * D])
        w_in = kb.dram_in("ev_w_in", [D, 4 * D])
        biasG_d = kb.dram_in("biasG", [128, 5 * 8 * 128])
        biasE_d = kb.dram_in("biasE", [4, 128, 5 * 8 * 128])
        out_a = kb.dram_out("out_a", [128, 4, SEG], BF16)
        out_hg = kb.dram_out("out_hg", [20, 128, SEG])
        out_chg = kb.dram_out("out_chg", [12, 128, 256])

        QT = kb.sb("QT", [128, 4, SEG], BF16)
        qtok = Tok()
        KT = kb.sb("KT", [128, 4, NS1], BF16)
        ktok = Tok()
        KcT = kb.sb("KcT", [128, 4, 256], BF16)
        kctok = Tok()
        V = kb.sb("V", [128, 22, 512], BF16)
        vtok = Tok()
        SCR = 24576
        scr = kb.sb("scr", [128, SCR], F32)
        cv = Carver(scr)

        ident, itok = kb.load_small("ident_sb", ident_d[:, :], [128, 128])
        c2, c2tok = kb.load_small("c2_sb", c2_d[:, :, :], [128, 8, 2])
        ada_b, abtok = kb.load_small("adab", ada_b_d[:, :], [128, 48])
        nw, nwtok = kb.load_small("nw_sb", nw_d[:, :], [128, 8])
        onesb = kb.sb("onesb", [128, 128], BF16)
        onesb_tok = Tok()
        P.I("pool", "memset", [], [onesb_tok], ap=onesb[:], constant=1.0)

        scT = kb.sb("scT", [128, 8, 2], F32)
        sctok = Tok()
        P.I("act", "activation", [c2tok], [sctok], out=scT[:, :, :], in_=c2[:, :, :], func=AF.Silu)
        mod = kb.sb("mod", [128, 2, 48], F32)
        modtok = Tok()
        compute_mod(kb, ada_w, ada_b, abtok, scT, sctok, 2, list(range(0, 16)), mod, modtok)
        weff = kb.sb("weff", [128, 2, 8], F32)
        wefftok = Tok()
        for r in range(2):
            P.I("dve", "scalar_tensor_tensor", [modtok, nwtok], [wefftok], out=weff[:, r, :], in0=mod[:, r, 8:16], scalar=1.0, in1=nw[:, :], op0=ALU.add, op1=ALU.mult)

        hT = cv.take(8 * NS1, BF16).rearrange("p (k n) -> p k n", k=8)
        htok = Tok()
        hcT = cv.take(8 * 256, BF16).rearrange("p (k n) -> p k n", k=8)
        hctok = Tok()
        xTb = cv.take(8 * 512, F32).rearrange("p (k n) -> p k n", k=8)
        xbtok = Tok()
        Wv = cv.take(8 * 512, BF16).rearrange("p (k n) -> p k n", k=8)
        wvtok = Tok()
        xt = [cv.take(1024, F32) for _ in range(2)]
        xtt = [Tok(), Tok()]
        stg = [cv.take(512, F32) for _ in range(3)]
        stgt = [Tok() for _ in range(3)]

        for blk in range(5):
            load_xT(kb, x_s1[blk * 512:(blk + 1) * 512, :], 512, xTb, xbtok, ident, itok, xt, xtt)
            norm_mod(kb, xTb, [xbtok], 0, 512, weff[:, 0, :], mod[:, 0, 0:8], [wefftok, modtok], hT, [htok], hoff=blk * 512)
        load_xT(kb, ctx_d, 256, xTb, xbtok, ident, itok, xt, xtt)
        norm_mod(kb, xTb, [xbtok], 0, 256, weff[:, 1, :], mod[:, 1, 0:8], [wefftok, modtok], hcT, [hctok], hoff=0)

        ev = [0]

        def evac(dst, src, rd, wr):
            ev[0] += 1
            if ev[0] % 2 == 0:
                P.I("act", "copy", rd, wr, out=dst, in_=src)
            else:
                P.I("dve", "tensor_copy", rd, wr, out=dst, in_=src)

        own_sbs = [(OWN0 + i * 512, OWN0 + (i + 1) * 512) for i in range(4)]
        all_sbs = [(i * 512, (i + 1) * 512) for i in range(5)]
        for n in range(32):
            grp = n // 4
            if grp == 2:
                continue
            w, wt = kb.load_w(w_in, 0, 8, n * 128, 128)
            sbs = all_sbs if grp == 1 else own_sbs
            for (s0, s1) in sbs:
                bk, bt = kb.bank()
                kb.mm_acc(bk[:, 0:512], bt, [(w[:, kc, :], hT[:, kc, s0:s1]) for kc in range(8)], [wt, htok])
                if grp == 0:
                    evac(QT[:, n, s0 - OWN0:s1 - OWN0], bk[:, 0:512], [bt], [qtok])
                elif grp == 1:
                    evac(KT[:, n - 4, s0:s1], bk[:, 0:512], [bt], [ktok])
                else:
                    i = kb.rot("stg", 3)
                    evac(stg[i][:, 0:512], bk[:, 0:512], [bt], [stgt[i]])
                    P.dma(out_hg[n - 12, :, s0 - OWN0:s1 - OWN0], stg[i][:, 0:512], reads=[stgt[i]])
            if grp == 1 or grp in (4, 5, 6):
                bk, bt = kb.bank()
                kb.mm_acc(bk[:, 0:256], bt, [(w[:, kc, :], hcT[:, kc, :]) for kc in range(8)], [wt, hctok])
                if grp == 1:
                    evac(KcT[:, n - 4, :], bk[:, 0:256], [bt], [kctok])
                else:
                    i = kb.rot("stg", 3)
                    evac(stg[i][:, 0:256], bk[:, 0:256], [bt], [stgt[i]])
                    P.dma(out_chg[n - 16, :, :], stg[i][:, 0:256], reads=[stgt[i]])
        for c in range(4):
            i = kb.rot("wst", len(kb.wst))
            stgw = kb.wst[i][:, 0:1024].rearrange("p (kc n) -> p kc n", kc=8)
            P.dma(stgw, w_in[:, 1024 + c * 128:1024 + (c + 1) * 128].rearrange("(kc p) n -> p kc n", p=128), writes=[kb.wst_tok[i]])
            P.I("act", "copy", [kb.wst_tok[i]], [wvtok], out=Wv[:, :, c * 128:(c + 1) * 128], in_=stgw)
        for s in range(22):
            bk, bt = kb.bank()
            if s < 20:
                pairs = [(hT[:, kc, s * 128:(s + 1) * 128], Wv[:, kc, :]) for kc in range(8)]
                rd = [htok, wvtok]
            else:
                pairs = [(hcT[:, kc, (s - 20) * 128:(s - 19) * 128], Wv[:, kc, :]) for kc in range(8)]
                rd = [hctok, wvtok]
            kb.mm_acc(bk[:, 0:512], bt, pairs, rd)
            evac(V[:, s, :], bk[:, 0:512], [bt], [vtok])

        P.barrier()
        cv.reset()
        biasG = cv.take(5120, F32).rearrange("p (c h q) -> p c h q", c=5, h=8)
        bgtok = Tok()
        biasE = cv.take(5120, F32)
        betok = Tok()
        sst = [cv.take(512, F32) for _ in range(4)]
        sstt = [Tok() for _ in range(4)]
        PT = [cv.take(7 * 1024, BF16).rearrange("p (c n) -> p c n", c=7) for _ in range(2)]
        pttok = [Tok(), Tok()]
        rD = [cv.take(512, F32) for _ in range(2)]
        rdtok = [Tok(), Tok()]
        AT = cv.take(4 * SEG, BF16).rearrange("p (k n) -> p k n", k=4)
        attok = Tok()
        assert cv.off <= SCR
        P.dma(biasG.rearrange("p c h q -> p (c h q)"), biasG_d[:, :], writes=[bgtok])

        S_banks = [(kb.banks[i], kb.btok[i]) for i in range(4)]
        O_banks = [(kb.banks[4], kb.btok[4]), (kb.banks[5], kb.btok[5])]
        D_banks = [(kb.banks[6], kb.btok[6]), (kb.banks[7], kb.btok[7])]
        sri = [0]
        edge_of = {0: 0, 1: 1, 14: 2, 15: 3}

        def qk(t):
            pb = t % 2
            if t in edge_of:
                P.dma(biasE, biasE_d[edge_of[t], :, :], writes=[betok])
                btab = biasE.rearrange("p (c h q) -> p c h q", c=5, h=8)
                btk = betok
            else:
                btab, btk = biasG, bgtok
            for ci in range(7):
                for half in range(2):
                    bk, bt = S_banks[sri[0]]
                    sri[0] = (sri[0] + 1) % 4
                    for hh in range(4):
                        h = 2 * hh + half
                        pr = half * 64
                        if ci < 5:
                            lhsT = KT[pr:pr + 64, h // 2, (t + ci) * 128:(t + ci + 1) * 128]
                            rd = [ktok, qtok]
                        else:
                            lhsT = KcT[pr:pr + 64, h // 2, (ci - 5) * 128:(ci - 4) * 128]
                            rd = [kctok, qtok]
                        P.I("pe", "matmul", rd, [bt], inc=(hh == 3), out=bk[:, hh * 128:(hh + 1) * 128], lhsT=lhsT,
                            rhs=QT[pr:pr + 64, h // 2, t * 128:(t + 1) * 128], start=True, stop=True)
                    dst = PT[pb][:, ci, half * 512:(half + 1) * 512]
                    if ci < 5:
                        si = kb.rot("sst", 4)
                        P.I("dve", "scalar_tensor_tensor", [bt, btk], [sstt[si]], out=sst[si].rearrange("p (h q) -> p h q", h=4),
                            in0=bk[:, :].rearrange("p (h q) -> p h q", h=4), scalar=0.125, in1=btab[:, ci, half * 4:half * 4 + 4, :], op0=ALU.mult, op1=ALU.add)
                        P.I("act", "activation", [sstt[si]], [pttok[pb]], out=dst, in_=sst[si], func=AF.Exp)
                    else:
                        P.I("act", "activation", [bt], [pttok[pb]], out=dst, in_=bk[:, :], func=AF.Exp, scale=0.125)

        def pv(t):
            pb = t % 2
            for half in range(2):
                ob, ot = O_banks[half]
                for hh in range(4):
                    for ci in range(7):
                        vt = (t + ci) if ci < 5 else (20 + ci - 5)
                        P.I("pe", "matmul", [vtok, pttok[pb]], [ot], inc=(ci == 6), out=ob[:, hh * 128:(hh + 1) * 128],
                            lhsT=V[:, vt, hh * 128:(hh + 1) * 128], rhs=PT[pb][:, ci, half * 512 + hh * 128:half * 512 + (hh + 1) * 128], start=(ci == 0), stop=(ci == 6))
                db, dt_ = D_banks[half]
                for ci in range(7):
                    P.I("pe", "matmul", [onesb_tok, pttok[pb]], [dt_], inc=(ci == 6), out=db[:, :], lhsT=onesb[:, :],
                        rhs=PT[pb][:, ci, half * 512:(half + 1) * 512], start=(ci == 0), stop=(ci == 6))
                P.I("dve", "reciprocal", [dt_], [rdtok[half]], out=rD[half], in_=db[:, :])
                pr = half * 64
                ov = ob[pr:pr + 64, :].rearrange("p (j q) -> p j q", j=4)
                rv = rD[half][pr:pr + 64, :].rearrange("p (j q) -> p j q", j=4)
                P.I("dve", "tensor_tensor", [ot, rdtok[half]], [attok], out=AT[pr:pr + 64, :, t * 128:(t + 1) * 128], in0=ov, in1=rv, op=ALU.mult)

        if do_na:
            qk(0)
        for t in range(16 if do_na else 0):
            if t + 1 < 16:
                qk(t + 1)
            pv(t)
        if not do_na:
            P.I("pool", "memset", [], [attok], ap=AT[:, :, :], constant=0.0)
        for k in range(4):
            P.dma(out_a[:, k, :], AT[:, k, :], reads=[attok])
        P.finish("sp")
        P.emit()
    return nc


def _slot_rows(j, s):
    R0 = j * 32
    r = R0 - 4 + 2 * s
    if r < 0:
        return (6, 7) if s == 0 else None
    if r >= 128:
        return (120, 121) if s == 18 else None
    return (r, r + 1)


def _bias_table(j, t, rpb):
    R0 = j * 32
    tab = np.full((2, 64, 5, 8, 2, 64), NEG, np.float32)
    c = np.arange(64)
    cs = np.clip(c - 8, 0, 48)
    kc = np.arange(64)
    colvalid = (kc[:, None] >= cs[None, :]) & (kc[:, None] < cs[None, :] + 16)
    colrel = np.clip(kc[:, None] - c[None, :] + 15, 0, 30)
    used = [set(), set()]
    for p in range(5):
        rows = _slot_rows(j, t + p)
        if rows is None:
            continue
        for kpar in range(2):
            kr = rows[kpar]
            for qpar in range(2):
                qr = R0 + 2 * t + qpar
                r0 = min(max(qr - 4, 0), 120)
                if not (r0 <= kr < r0 + 8) or kr in used[qpar]:
                    continue
                used[qpar].add(kr)
                rr = kr - qr + 7
                vals = rpb[:, rr, :][:, colrel]
                vals = np.where(colvalid[None], vals, NEG)
                tab[kpar, :, p, :, qpar, :] = vals.transpose(1, 0, 2)
    assert all(len(u) == 8 for u in used), (j, t, used)
    tab = tab[:, :, :, [0, 2, 4, 6, 1, 3, 5, 7]]
    return np.ascontiguousarray(tab).reshape(128, 5 * 8 * 128)


def s1_inputs(core, inp):
    b, j = core // 4, core % 4
    x = inp["x"]
    x_s1 = np.zeros((NS1, D), np.float32)
    for s in range(20):
        rows = _slot_rows(j, s)
        if rows is not None:
            x_s1[s * 128:(s + 1) * 128] = x[b, rows[0] * 64:rows[0] * 64 + 128]
    rpb = np.asarray(inp["na_rpb"][0], np.float32)
    c2 = np.ascontiguousarray(np.stack([fm(inp["c"][b], 8), fm(inp["c_ctx"], 8)], axis=2))
    return {
        "x_s1": x_s1, "ctx": np.ascontiguousarray(inp["ctx"][b]), "c2": c2,
        "ident": np.eye(128, dtype=np.float32),
        "ada_b0": fm(inp["ada_b"][0], 48), "nw0": fm(inp["norm_mix_w"][0], 8),
        "ada_w0": np.asarray(inp["ada_w"][0]), "ev_w_in": np.asarray(inp["ev_w_in"][0]),
        "biasG": _bias_table(1, 5, rpb),
        "biasE": np.ascontiguousarray(np.stack([_bias_table(j, t, rpb) for t in (0, 1, 14, 15)], axis=0)),
    }


def build_s2():
    nc = bass.Bass("TRN2", target_bir_lowering=False)
    with ExitStack() as st:
        kb = KB(nc, st, n_wst=1, n_wbf=1)
        P = kb.P
        qT_d = kb.dram_in("qT", [128, SEQ])
        fT_d = kb.dram_in("fT", [2, 128, SEQ])
        ftok_d = kb.dram_in("ftok", [2, SEQ, 128])
        itok_d = kb.dram_in("itok", [SEQ, 128])
        gT_d = kb.dram_in("gT", [128, SEQ])
        cftok_d = kb.dram_in("cftok", [2, 256, 128])
        citok_d = kb.dram_in("citok", [256, 128])
        lbT_d = kb.dram_in("lbT_in", [128, 2, 2])
        lbrow_d = kb.dram_in("lbrow_in", [128, 2, 2, 128])
        normw_d = kb.dram_in("normw", [128, 1])
        masks_d = kb.dram_in("masks", [128, 4, 128])
        ci_d = kb.dram_in("ci", [128, 4])
        yT_d = kb.dram_out("yT", [128, SEQ], BF16)

        masks, mtok = kb.load_small("masks_sb", masks_d[:, :, :], [128, 4, 128])
        CI, citok = kb.load_small("ci_sb", ci_d[:, :], [128, 4])
        lbl, lbltok = kb.load_small("lbl_sb", lbT_d[:, :, :], [128, 2, 2])
        lbr, lbrtok = kb.load_small("lbr_sb", lbrow_d[:, :, :, :], [128, 2, 2, 128])
        normw, nwtok = kb.load_small("normw_sb", normw_d[:, :], [128, 1])

        lbT = kb.sb("lbT", [128, 2], F32)
        omlT = kb.sb("omlT", [128, 2], F32)
        lbrow = kb.sb("lbrow", [128, 2, 128], F32)
        omlrow = kb.sb("omlrow", [128, 2, 128], F32)
        lbtok = Tok()
        P.I("dve", "tensor_tensor", [lbltok], [lbtok], out=lbT[:, :], in0=lbl[:, :, 0], in1=lbl[:, :, 1], op=ALU.subtract)
        P.I("act", "activation", [lbtok], [lbtok], out=lbT[:, :], in_=lbT[:, :], func=AF.Sigmoid)
        P.I("dve", "tensor_scalar", [lbtok], [lbtok], out=omlT[:, :], in0=lbT[:, :], scalar1=-0.5, scalar2=0.5, op0=ALU.mult, op1=ALU.add)
        P.I("dve", "tensor_scalar", [lbtok], [lbtok], out=lbT[:, :], in0=lbT[:, :], scalar1=0.5, scalar2=0.5, op0=ALU.mult, op1=ALU.add)
        P.I("dve", "tensor_tensor", [lbrtok], [lbtok], out=lbrow[:, :, :], in0=lbr[:, :, 0, :], in1=lbr[:, :, 1, :], op=ALU.subtract)
        P.I("act", "activation", [lbtok], [lbtok], out=lbrow[:, :, :], in_=lbrow[:, :, :], func=AF.Sigmoid)
        P.I("dve", "tensor_scalar", [lbtok], [lbtok], out=omlrow[:, :, :], in0=lbrow[:, :, :], scalar1=-0.5, scalar2=0.5, op0=ALU.mult, op1=ALU.add)
        P.I("dve", "tensor_scalar", [lbtok], [lbtok], out=lbrow[:, :, :], in0=lbrow[:, :, :], scalar1=0.5, scalar2=0.5, op0=ALU.mult, op1=ALU.add)

        obwd = kb.sb("obwd", [128, SEQ], F32)
        obtok = Tok()
        S = kb.sb("S", [128, 128], F32)
        stok = Tok()
        NSB = 6
        Sb = [kb.sb("Sb%d" % i, [128, 128], BF16) for i in range(NSB)]
        sbtok = [Tok() for _ in range(NSB)]
        ybuf = [kb.sb("ybuf%d" % i, [128, 512], BF16) for i in range(2)]
        ybtok = [Tok(), Tok()]

        tmps = {}

        def tmp(name, width=128, dt=F32, n=2):
            if name not in tmps:
                tmps[name] = ([kb.sb("%s_%d" % (name, i), [128, width], dt) for i in range(n)], [Tok() for _ in range(n)])
            bufs, toks = tmps[name]
            i = kb.rot("tmp_" + name, n)
            return bufs[i], toks[i]

        grp_bufs = {}
        for nm, shape in (("q", [128, 512]), ("fT", [128, 512]), ("ft", [128, 4, 128]), ("it", [128, 4, 128]), ("g", [128, 512])):
            grp_bufs[nm] = ([kb.sb("grp_%s%d" % (nm, i), shape, F32) for i in range(2)], [Tok(), Tok()])

        def load_group(g, d, full, ctx):
            out = {}
            i = kb.rot("grp", 2)
            if ctx:
                b, t = grp_bufs["ft"][0][i], grp_bufs["ft"][1][i]
                P.dma(b[:, 0:2, :], cftok_d[d].rearrange("(n p) k -> p n k", p=128), writes=[t])
                out["ft"] = (b, t)
                b, t = grp_bufs["it"][0][i], grp_bufs["it"][1][i]
                P.dma(b[:, 0:2, :], citok_d.rearrange("(n p) k -> p n k", p=128), writes=[t])
                out["it"] = (b, t)
                return out
            c0 = g * 512
            b, t = grp_bufs["ft"][0][i], grp_bufs["ft"][1][i]
            P.dma(b[:, :, :], ftok_d[d, c0:c0 + 512, :].rearrange("(n p) k -> p n k", p=128), writes=[t])
            out["ft"] = (b, t)
            b, t = grp_bufs["it"][0][i], grp_bufs["it"][1][i]
            P.dma(b[:, :, :], itok_d[c0:c0 + 512, :].rearrange("(n p) k -> p n k", p=128), writes=[t])
            out["it"] = (b, t)
            b, t = grp_bufs["q"][0][i], grp_bufs["q"][1][i]
            P.dma(b[:, :], qT_d[:, c0:c0 + 512], writes=[t])
            out["q"] = (b, t)
            b, t = grp_bufs["fT"][0][i], grp_bufs["fT"][1][i]
            P.dma(b[:, :], fT_d[d, :, c0:c0 + 512], writes=[t])
            out["fT"] = (b, t)
            if d == 0:
                b, t = grp_bufs["g"][0][i], grp_bufs["g"][1][i]
                P.dma(b[:, :], gT_d[:, c0:c0 + 512], writes=[t])
                out["g"] = (b, t)
            return out

        sb_cur = [0]

        def tile(d, gi, ti, full, tok0):
            Mi = masks[:, 2 * d, :]
            Mx = masks[:, 2 * d + 1, :]
            ftb, ftt = gi["ft"]
            itb, itt = gi["it"]
            sg, sgt = tmp("sg")
            P.I("act", "activation", [ftt], [sgt], out=sg[:, :], in_=ftb[:, ti, :], func=AF.Tanh, scale=0.5)
            f, ft_ = tmp("f")
            P.I("dve", "tensor_tensor", [sgt, lbtok], [ft_], out=f[:, :], in0=sg[:, :], in1=omlrow[:, d, :], op=ALU.mult)
            P.I("dve", "tensor_tensor", [ft_, lbtok], [ft_], out=f[:, :], in0=f[:, :], in1=lbrow[:, d, :], op=ALU.add)
            lf, lft = tmp("lf")
            P.I("act", "activation", [ft_], [lft], out=lf[:, :], in_=f[:, :], func=AF.Ln)
            kt, ktt = tmp("kt")
            P.I("pool", "tensor_scalar", [ft_], [ktt], out=kt[:, :], in0=f[:, :], scalar1=-1.0, scalar2=1.0, op0=ALU.mult, op1=ALU.add)
            vb, vbt = tmp("vb", 128, BF16)
            P.I("pool", "tensor_copy", [itt], [vbt], out=vb[:, :], in_=itb[:, ti, :])
            bk, bt = kb.bank()
            P.I("pe", "matmul", [mtok, lft], [bt], out=bk[:, 0:128], lhsT=Mx, rhs=lf[:, :], start=True, stop=True)
            eD, eDt = tmp("eD")
            P.I("act", "activation", [bt], [eDt], out=eD[:, :], in_=bk[:, 0:128], func=AF.Exp)
            P.I("dve", "tensor_tensor", [eDt, ktt], [eDt], out=eD[:, :], in0=eD[:, :], in1=kt[:, :], op=ALU.mult)
            kdm, kdmt = tmp("kdm", 512, BF16)
            for c in range(4):
                P.I("dve", "tensor_scalar", [eDt, citok], [kdmt], out=kdm[:, c * 128:(c + 1) * 128], in0=eD[:, :], scalar1=CI[:, c:c + 1], scalar2=None, op0=ALU.mult)
            bk2, bt2 = kb.bank()
            P.I("pe", "matmul", [citok, lft], [bt2], out=bk2[:, 0:4], lhsT=lf[:, :], rhs=CI[:, :], start=True, stop=True)
            ebt, ebtt = tmp("ebt", 4)
            P.I("act", "activation", [bt2], [ebtt], out=ebt[:, :], in_=bk2[:, 0:4], func=AF.Exp)
            bkS, btS = kb.bank()
            for c in range(4):
                P.I("pe", "matmul", [kdmt, vbt], [btS], inc=(c == 3), out=bkS[:, c * 128:(c + 1) * 128], lhsT=kdm[:, c * 128:(c + 1) * 128], rhs=vb[:, :], start=True, stop=True)
            corder = [0, 1, 2, 3] if d == 0 else [3, 2, 1, 0]
            if full:
                qb, qt_ = gi["q"]
                fTb, fTt = gi["fT"]
                sgT, sgTt = tmp("sgT")
                P.I("act", "activation", [fTt], [sgTt], out=sgT[:, :], in_=fTb[:, ti * 128:(ti + 1) * 128], func=AF.Tanh, scale=0.5)
                P.I("dve", "tensor_scalar", [sgTt, lbtok], [sgTt], out=sgT[:, :], in0=sgT[:, :], scalar1=omlT[:, d:d + 1], scalar2=lbT[:, d:d + 1], op0=ALU.mult, op1=ALU.add)
                kT, kTt = tmp("kT")
                P.I("pool", "tensor_scalar", [sgTt], [kTt], out=kT[:, :], in0=sgT[:, :], scalar1=-1.0, scalar2=1.0, op0=ALU.mult, op1=ALU.add)
                bk3, bt3 = kb.bank()
                P.I("pe", "matmul", [mtok, lft], [bt3], out=bk3[:, 0:128], lhsT=lf[:, :], rhs=Mi, start=True, stop=True)
                eB, eBt = tmp("eB")
                P.I("act", "activation", [bt3], [eBt], out=eB[:, :], in_=bk3[:, 0:128], func=AF.Exp)
                enB, enBt = tmp("enB")
                P.I("act", "activation", [bt3], [enBt], out=enB[:, :], in_=bk3[:, 0:128], func=AF.Exp, scale=-1.0)
                qs, qst = tmp("qs")
                P.I("act", "activation", [qt_], [qst], out=qs[:, :], in_=qb[:, ti * 128:(ti + 1) * 128], func=AF.Tanh, scale=0.5)
                P.I("dve", "scalar_tensor_tensor", [qst, qt_], [qst], out=qs[:, :], in0=qs[:, :], scalar=1.0, in1=qb[:, ti * 128:(ti + 1) * 128], op0=ALU.add, op1=ALU.mult)
                qe, qet = tmp("qe", 128, BF16)
                P.I("dve", "scalar_tensor_tensor", [qst, eBt], [qet], out=qe[:, :], in0=qs[:, :], scalar=0.5, in1=eB[:, :], op0=ALU.mult, op1=ALU.mult)
                ke, ket = tmp("ke", 128, BF16)
                P.I("pool", "tensor_tensor", [kTt, enBt], [ket], out=ke[:, :], in0=kT[:, :], in1=enB[:, :], op=ALU.mult)
                bk4, bt4 = kb.bank()
                P.I("pe", "matmul", [ket, qet], [bt4], out=bk4[:, 0:128], lhsT=ke[:, :], rhs=qe[:, :], start=True, stop=True)
                am, amt = tmp("am", 128, BF16)
                P.I("dve", "tensor_tensor", [bt4, mtok], [amt], out=am[:, :], in0=bk4[:, 0:128], in1=Mi, op=ALU.mult)
                bkO, btO = kb.bank()
                P.I("pe", "matmul", [vbt, amt], [btO], inc=False, out=bkO[:, 0:128], lhsT=vb[:, :], rhs=am[:, :], start=True, stop=False, skip_group_check=True)
            for n_, c in enumerate(corder):
                if full:
                    cur = sb_cur[0]
                    P.I("pe", "matmul", [sbtok[cur], qet], [btO], inc=(n_ == 3), out=bkO[:, c * 32:(c + 1) * 32],
                        lhsT=Sb[cur][:, :], rhs=qe[:, c * 32:(c + 1) * 32], start=False, stop=(n_ == 3), skip_group_check=True)
                P.I("dve", "scalar_tensor_tensor", [btS, ebtt, stok], [stok], out=S[:, :], in0=S[:, :], scalar=ebt[:, c:c + 1], in1=bkS[:, c * 128:(c + 1) * 128], op0=ALU.mult, op1=ALU.add)
                if full or (n_ == 3):
                    nxt = (sb_cur[0] + 1) % NSB
                    P.I("pool", "tensor_copy", [stok], [sbtok[nxt]], out=Sb[nxt][:, :], in_=S[:, :])
                    sb_cur[0] = nxt
            if not full:
                return
            if d == 1:
                P.I("act", "copy", [btO], [obtok], out=obwd[:, tok0:tok0 + 128], in_=bkO[:, 0:128])
                return
            gb, gt_ = gi["g"]
            o, ot = tmp("o")
            P.I("dve", "tensor_tensor", [btO, obtok], [ot], out=o[:, :], in0=bkO[:, 0:128], in1=obwd[:, tok0:tok0 + 128], op=ALU.add)
            sq, sqt = tmp("sq")
            P.I("act", "activation", [ot], [sqt], out=sq[:, :], in_=o[:, :], func=AF.Square)
            bk5, bt5 = kb.bank()
            P.I("pe", "matmul", [sqt, kb.ones_tok], [bt5], out=bk5[:, 0:128], lhsT=kb.ones32[:], rhs=sq[:, :], start=True, stop=True)
            rs, rst = tmp("rs")
            P.I("act", "activation", [bt5, kb.eps_tok], [rst], out=rs[:, :], in_=bk5[:, 0:128], func=AF.Sqrt, bias=kb.eps_sb[:, 0:1], scale=1.0 / 128)
            P.I("dve", "reciprocal", [rst], [rst], out=rs[:, :], in_=rs[:, :])
            P.I("dve", "scalar_tensor_tensor", [ot, rst, nwtok], [ot], out=o[:, :], in0=o[:, :], scalar=normw[:, 0:1], in1=rs[:, :], op0=ALU.mult, op1=ALU.mult)
            sl, slt = tmp("sl")
            P.I("act", "activation", [gt_], [slt], out=sl[:, :], in_=gb[:, ti * 128:(ti + 1) * 128], func=AF.Tanh, scale=0.5)
            P.I("dve", "scalar_tensor_tensor", [slt, gt_], [slt], out=sl[:, :], in0=sl[:, :], scalar=1.0, in1=gb[:, ti * 128:(ti + 1) * 128], op0=ALU.add, op1=ALU.mult)
            yb = (tok0 // 512) % 2
            P.I("dve", "scalar_tensor_tensor", [ot, slt], [ybtok[yb]], out=ybuf[yb][:, ti * 128:(ti + 1) * 128], in0=o[:, :], scalar=0.5, in1=sl[:, :], op0=ALU.mult, op1=ALU.mult)
            if ti == 3:
                c0 = (tok0 // 512) * 512
                P.dma(yT_d[:, c0:c0 + 512], ybuf[yb][:, :], reads=[ybtok[yb]])

        for d in (1, 0):
            P.I("pool", "memset", [stok], [stok], ap=S[:, :], constant=0.0)
            gi = load_group(0, d, False, True)
            for ti in ([0, 1] if d == 0 else [1, 0]):
                tile(d, gi, ti, False, 0)
            groups = list(range(16)) if d == 0 else list(range(15, -1, -1))
            for g in groups:
                gi = load_group(g, d, True, False)
                for ti in ([0, 1, 2, 3] if d == 0 else [3, 2, 1, 0]):
                    tile(d, gi, ti, True, g * 512 + ti * 128)
        P.finish("sp")
        P.emit()
    return nc


def _hg_masks():
    s = np.arange(128)[:, None]
    t = np.arange(128)[None, :]
    same = (s // 32) == (t // 32)
    Mi_f = (same & (s <= t)).astype(np.float32)
    Mx_f = (same & (s > t)).astype(np.float32)
    m = np.stack([Mi_f, Mx_f, Mi_f.T, Mx_f.T], axis=1)
    ci = (np.arange(128)[:, None] // 32 == np.arange(4)[None, :]).astype(np.float32)
    return np.ascontiguousarray(m), np.ascontiguousarray(ci)


def s2_inputs(core, inp, hg_full, chg_full):
    b, hd = core // 4, core % 4
    sl = lambda grp: slice(grp * 512 + hd * 128, grp * 512 + (hd + 1) * 128)
    q = hg_full[b][:, sl(0)]
    ff = hg_full[b][:, sl(1)]
    fb = hg_full[b][:, sl(2)]
    iv = hg_full[b][:, sl(3)]
    g = hg_full[b][:, sl(4)]
    lbl = np.asarray(inp["hg_lb_logits"], np.float32)[:, :, hd * 128:(hd + 1) * 128]
    masks, ci = _hg_masks()
    return {
        "qT": np.ascontiguousarray(q.T), "fT": np.ascontiguousarray(np.stack([ff.T, fb.T], axis=0)),
        "ftok": np.ascontiguousarray(np.stack([ff, fb], axis=0)), "itok": np.ascontiguousarray(iv),
        "gT": np.ascontiguousarray(g.T),
        "cftok": np.ascontiguousarray(np.stack([chg_full[b][:, hd * 128:(hd + 1) * 128], chg_full[b][:, 512 + hd * 128:512 + (hd + 1) * 128]], axis=0)),
        "citok": np.ascontiguousarray(chg_full[b][:, 1024 + hd * 128:1024 + (hd + 1) * 128]),
        "lbT_in": np.ascontiguousarray(lbl.transpose(2, 1, 0)),
        "lbrow_in": np.ascontiguousarray(np.broadcast_to(lbl.transpose(1, 0, 2)[None], (128, 2, 2, 128))),
        "normw": np.ascontiguousarray(np.asarray(inp["hg_norm_w"][0], np.float32)[hd * 128:(hd + 1) * 128].reshape(128, 1)),
        "masks": masks, "ci": ci,
    }


def kernel(**inputs):
    inp = {k: np.asarray(v) for k, v in inputs.items()}
    cores = list(range(NCORES))
    nc1 = build_s1()
    r1 = run_bass_kernel_spmd(nc1, [s1_inputs(c, inp) for c in cores], core_ids=cores).results
    hg_full = np.zeros((2, SEQ, 2560), np.float32)
    chg_full = np.zeros((2, 256, 1536), np.float32)
    AG = np.zeros((2, SEQ, D), np.float32)
    for c in cores:
        b, j = c // 4, c % 4
        hg_full[b, j * SEG:(j + 1) * SEG] = np.asarray(r1[c]["out_hg"]).transpose(2, 0, 1).reshape(SEG, 2560)
        if j == 0:
            chg_full[b] = np.asarray(r1[c]["out_chg"]).transpose(2, 0, 1).reshape(256, 1536)
        AG[b, j * SEG:(j + 1) * SEG, 0:512] = np.asarray(r1[c]["out_a"]).transpose(2, 1, 0).reshape(SEG, 512)
    nc2 = build_s2()
    r2 = run_bass_kernel_spmd(nc2, [s2_inputs(c, inp, hg_full, chg_full) for c in cores], core_ids=cores).results
    for c in cores:
        b, hd = c // 4, c % 4
        AG[b, :, 512 + hd * 128:512 + (hd + 1) * 128] = np.asarray(r2[c]["yT"]).T
    nc3 = build_s3()
    r3 = run_bass_kernel_spmd(nc3, [s3_inputs(c, inp, AG) for c in cores], core_ids=cores).results
    out = np.zeros((2, SEQ, D), np.float32)
    for c in cores:
        b, j = c // 4, c % 4
        out[b, j * SEG:(j + 1) * SEG] = np.asarray(r3[c]["out"])
    return out
```
